# Optimizing a Trainium2 kernel written in Bass

```python
import math
import jax, jax.numpy as jnp
from jax import lax
import numpy as np

D_MODEL = 1024
BATCH = 8
SEQ = 2048
DEPTH = 4

CHUNK = 64
N_A = DEPTH // 2
N_B = DEPTH - N_A
CONV_W = 3
D_FF = 4 * D_MODEL
N_HEADS = 8
QK_NOPE = 128
QK_ROPE = 64
V_HEAD = 128
Q_LORA = 384
KV_LORA = 256
ROPE_THETA = 10000.0
Q_BLOCK = 128
NORM_EPS = 1e-6

kernel_name = "hybrid_shortconv_mla_yoco_trunk"


def rmsnorm(x, g):
    xf = x.astype(jnp.float32)
    y = xf * lax.rsqrt(jnp.mean(xf * xf, axis=-1, keepdims=True) + NORM_EPS)
    return (y * g.astype(jnp.float32)).astype(x.dtype)


def modulate(h, shift, scale):
    return h * (1 + scale[:, None, :]) + shift[:, None, :]


def rope_tables(seq_len, dim, dtype):
    inv_freq = 1.0 / (ROPE_THETA ** (jnp.arange(0, dim, 2, dtype=jnp.float32) / dim))
    ang = jnp.arange(seq_len, dtype=jnp.float32)[:, None] * inv_freq[None, :]
    return jnp.cos(ang).astype(dtype), jnp.sin(ang).astype(dtype)


def apply_rope(x, cos, sin):
    half = x.shape[-1] // 2
    x1, x2 = x[..., :half], x[..., half:]
    return jnp.concatenate([x1 * cos - x2 * sin, x2 * cos + x1 * sin], axis=-1)


def short_conv_mixer(h, w_in, w_conv, b_conv, w_out):
    bcx = h @ w_in
    b_gate, c_gate, xh = jnp.split(bcx, 3, axis=-1)
    u = c_gate * xh
    conv = lax.conv_general_dilated(
        u, w_conv[:, None, :], window_strides=(1,), padding=[(CONV_W - 1, 0)],
        dimension_numbers=("NWC", "WIO", "NWC"), feature_group_count=D_MODEL) + b_conv
    return (b_gate * conv) @ w_out


def squared_relu_mlp(h, w_up, w_down):
    return jnp.square(jax.nn.relu(h @ w_up)) @ w_down


def shared_kv(x, c_act, kv_ada_w, kv_ada_b, kv_norm_g, kv_a_w, kv_a_norm_g, kv_b_w, cos, sin):
    B, S, _ = x.shape
    shift, scale = jnp.split(c_act @ kv_ada_w + kv_ada_b, 2, axis=-1)
    hkv = modulate(rmsnorm(x, kv_norm_g), shift, scale)
    kv_a = hkv @ kv_a_w
    c_kv = rmsnorm(kv_a[..., :KV_LORA], kv_a_norm_g)
    k_pe = apply_rope(kv_a[..., KV_LORA:], cos, sin)
    kv = (c_kv @ kv_b_w).reshape(B, S, N_HEADS, QK_NOPE + V_HEAD)
    return kv[..., :QK_NOPE], kv[..., QK_NOPE:], k_pe


def mla_mixer(h, q_a_w, q_a_norm_g, q_b_w, o_w, k_nope, v, k_pe, cos, sin):
    B, S, _ = h.shape
    q = (rmsnorm(h @ q_a_w, q_a_norm_g) @ q_b_w).reshape(B, S, N_HEADS, QK_NOPE + QK_ROPE)
    sm_scale = 1.0 / math.sqrt(QK_NOPE + QK_ROPE)
    q_nope = q[..., :QK_NOPE] * sm_scale
    q_pe = apply_rope(q[..., QK_NOPE:], cos[:, None, :], sin[:, None, :]) * sm_scale
    outs = []
    for i in range(S // Q_BLOCK):
        q0, k_end = i * Q_BLOCK, (i + 1) * Q_BLOCK
        qn = q_nope[:, q0:k_end]
        qp = q_pe[:, q0:k_end]
        s = (jnp.einsum("bqhd,bkhd->bhqk", qn, k_nope[:, :k_end])
             + jnp.einsum("bqhr,bkr->bhqk", qp, k_pe[:, :k_end])).astype(jnp.float32)
        q_chunk = np.arange(q0, k_end) // CHUNK
        k_chunk = np.arange(k_end) // CHUNK
        mask = k_chunk[None, :] <= q_chunk[:, None]
        s = jnp.where(mask[None, None], s, jnp.finfo(jnp.float32).min)
        p = jax.nn.softmax(s, axis=-1).astype(v.dtype)
        outs.append(jnp.einsum("bhqk,bkhd->bqhd", p, v[:, :k_end]))
    o = jnp.concatenate(outs, axis=1).reshape(B, S, N_HEADS * V_HEAD)
    return o @ o_w


def setup_inputs(seed: int = 0) -> dict:
    key = jax.random.key(seed)
    ks = jax.random.split(key, 24)
    D = D_MODEL
    f32 = jnp.float32

    def w(k, shape, fan_in):
        return jax.random.normal(k, shape, f32) * (fan_in ** -0.5)

    def gain(k, shape):
        return 1.0 + 0.05 * jax.random.normal(k, shape, f32)

    return {
        "x": jax.random.normal(ks[0], (BATCH, SEQ, D), f32),
        "c": jax.random.normal(ks[1], (BATCH, D), f32),
        "ada_w": w(ks[2], (DEPTH, D, 6 * D), D) * 0.5,
        "ada_b": 0.02 * jax.random.normal(ks[3], (DEPTH, 6 * D), f32),
        "norm_g": gain(ks[4], (DEPTH, 4, D)),
        "conv_in_w": w(ks[5], (N_A, D, 3 * D), D),
        "conv_w": w(ks[6], (N_A, CONV_W, D), CONV_W),
        "conv_b": 0.02 * jax.random.normal(ks[7], (N_A, D), f32),
        "conv_out_w": w(ks[8], (N_A, D, D), D),
        "kv_ada_w": w(ks[9], (D, 2 * D), D) * 0.5,
        "kv_ada_b": 0.02 * jax.random.normal(ks[10], (2 * D,), f32),
        "kv_norm_g": gain(ks[11], (D,)),
        "kv_a_w": w(ks[12], (D, KV_LORA + QK_ROPE), D),
        "kv_a_norm_g": gain(ks[13], (KV_LORA,)),
        "kv_b_w": w(ks[14], (KV_LORA, N_HEADS * (QK_NOPE + V_HEAD)), KV_LORA),
        "q_a_w": w(ks[15], (N_B, D, Q_LORA), D),
        "q_a_norm_g": gain(ks[16], (N_B, Q_LORA)),
        "q_b_w": w(ks[17], (N_B, Q_LORA, N_HEADS * (QK_NOPE + QK_ROPE)), Q_LORA),
        "attn_o_w": w(ks[18], (N_B, N_HEADS * V_HEAD, D), N_HEADS * V_HEAD),
        "mlp_up_w": w(ks[19], (DEPTH, D, D_FF), D),
        "mlp_down_w": w(ks[20], (DEPTH, D_FF, D), D_FF),
    }


def reference(x, c, ada_w, ada_b, norm_g, conv_in_w, conv_w, conv_b, conv_out_w,
              kv_ada_w, kv_ada_b, kv_norm_g, kv_a_w, kv_a_norm_g, kv_b_w,
              q_a_w, q_a_norm_g, q_b_w, attn_o_w, mlp_up_w, mlp_down_w):
    S = x.shape[1]
    c_act = jax.nn.silu(c)
    cos, sin = rope_tables(S, QK_ROPE, x.dtype)
    k_nope = v = k_pe = None
    for l in range(DEPTH):
        mod = c_act @ ada_w[l] + ada_b[l]
        sh_m, sc_m, g_m, sh_f, sc_f, g_f = jnp.split(mod, 6, axis=-1)
        h = modulate(rmsnorm(x, norm_g[l, 0]), sh_m, sc_m)
        if l < N_A:
            y = short_conv_mixer(h, conv_in_w[l], conv_w[l], conv_b[l], conv_out_w[l])
        else:
            if l == N_A:
                k_nope, v, k_pe = shared_kv(x, c_act, kv_ada_w, kv_ada_b, kv_norm_g,
                                            kv_a_w, kv_a_norm_g, kv_b_w, cos, sin)
            j = l - N_A
            y = mla_mixer(h, q_a_w[j], q_a_norm_g[j], q_b_w[j], attn_o_w[j],
                          k_nope, v, k_pe, cos, sin)
        x = x + g_m[:, None, :] * rmsnorm(y, norm_g[l, 1])
        h = modulate(rmsnorm(x, norm_g[l, 2]), sh_f, sc_f)
        y = squared_relu_mlp(h, mlp_up_w[l], mlp_down_w[l])
        x = x + g_f[:, None, :] * rmsnorm(y, norm_g[l, 3])
    return x
```

```python
import numpy as np
import concourse.bass as bass
import concourse.mybir as mybir
from concourse.bass_utils import run_bass_kernel_spmd

F32 = mybir.dt.float32
BF16 = mybir.dt.bfloat16
AF = mybir.ActivationFunctionType
ALU = mybir.AluOpType

D = 1024
S = 2048
DEPTH = 4
N_A = 2
DFF = 4096
NH = 8
QK_NOPE = 128
QK_ROPE = 64
V_HEAD = 128
Q_LORA = 384
KV_LORA = 256
EPS = 1e-6
CHUNK = 64
DC = D // 128
GT = 512
NG = S // GT
NSLOT = 4
WSLOT = 4096
SM_SCALE = 1.0 / float(np.sqrt(QK_NOPE + QK_ROPE))
ATTACH_WAIT = True
NFILL = 0


class Buf:
    __slots__ = ("name", "w", "r")

    def __init__(self, name):
        self.name = name
        self.w = None
        self.r = []


class Prog:
    ENGS = ("pe", "act", "dve", "pool", "sp")

    def __init__(self):
        self.ops = {e: [] for e in self.ENGS}
        self.gseq = 0
        self.dma_val = {}

    def emit(self, eng, fn, reads=(), writes=(), signal=True, dma_sem=None):
        idx = len(self.ops[eng])
        if dma_sem is not None:
            v = self.dma_val.get(dma_sem, 0) + 16
            self.dma_val[dma_sem] = v
            tok = ("d", dma_sem, v)
        else:
            tok = ("c", eng, idx)
        deps = set()

        def same(t):
            return dma_sem is None and t[0] == "c" and t[1] == eng

        for b in reads:
            if b.w is not None:
                if not (same(b.w) and eng == "pe"):
                    deps.add(b.w)
        for b in writes:
            for t in b.r:
                if not (same(t) and eng == "pe"):
                    deps.add(t)
            if b.w is not None and not (same(b.w) and eng == "pe"):
                deps.add(b.w)
        for b in reads:
            b.r.append(tok)
        for b in writes:
            b.w = tok
            b.r = []
        self.ops[eng].append(dict(fn=fn, deps=deps, signal=signal, dma_sem=dma_sem,
                                  gseq=self.gseq))
        self.gseq += 1
        return tok

    def lower(self, nc):
        engsem = {e: nc.alloc_semaphore(f"s_{e}") for e in ("pe", "act", "dve", "pool")}
        dmasem = {k: nc.alloc_semaphore(f"d_{k}") for k in self.dma_val}
        cum = {}
        nxt = {}
        for e in ("pe", "act", "dve", "pool"):
            ops = self.ops[e]
            if ops:
                ops[-1]["signal"] = True
            c = 0
            cl = []
            for o in ops:
                if o["signal"] and o["dma_sem"] is None and o["fn"] is not None:
                    c += 1
                cl.append(c)
            cum[e] = cl
            n = [None] * len(ops)
            last = None
            for i in range(len(ops) - 1, -1, -1):
                o = ops[i]
                if o["signal"] and o["dma_sem"] is None and o["fn"] is not None:
                    last = i
                n[i] = last
            nxt[e] = n

        def resolve(tok, gseq):
            if tok[0] == "d":
                return dmasem[tok[1]], tok[2], dmaop[(tok[1], tok[2])]
            e, i = tok[1], tok[2]
            j = nxt[e][i]
            assert j is not None, ("no signal after", tok)
            assert self.ops[e][j]["gseq"] < gseq, ("forward wait", tok, j)
            return engsem[e], cum[e][j], self.ops[e][j]

        allops = []
        for e in self.ENGS:
            for i, o in enumerate(self.ops[e]):
                o["eng"] = e
                o["idx"] = i
                allops.append(o)
        allops.sort(key=lambda x: x["gseq"])
        dmaop = {}
        dcount = {}
        for o in allops:
            ds = o["dma_sem"]
            if ds is not None:
                dcount[ds] = dcount.get(ds, 0) + 16
                o["dval"] = dcount[ds]
                dmaop[(ds, dcount[ds])] = o
        run_clock = {e: {} for e in self.ENGS}
        for o in allops:
            e = o["eng"]
            clk = run_clock[e]
            deps = []
            for t in o["deps"]:
                sem, val, prod = resolve(t, o["gseq"])
                deps.append((prod["gseq"], sem, val, prod))
            deps.sort(key=lambda x: -x[0])
            waits = []
            for _, sem, val, prod in deps:
                if clk.get(sem.num, 0) >= val:
                    continue
                waits.append((sem, val))
                for k, v in prod["tclock"].items():
                    if clk.get(k, 0) < v:
                        clk[k] = v
                if clk.get(sem.num, 0) < val:
                    clk[sem.num] = val
            o["waits"] = waits
            tc = dict(clk)
            if o["dma_sem"] is not None:
                tc[dmasem[o["dma_sem"]].num] = o["dval"]
            elif e != "sp" and o["fn"] is not None and o["signal"]:
                tc[engsem[e].num] = cum[e][o["idx"]]
            o["tclock"] = tc

        engobj = {"pe": "tensor", "act": "scalar", "dve": "vector", "pool": "gpsimd", "sp": "sync"}
        prog = self

        def run(ename, eng):
            for o in prog.ops[ename]:
                todo = list(o["waits"])
                attach = None
                if o["fn"] is not None and todo and ATTACH_WAIT:
                    attach = todo.pop(0)
                for sem, val in todo:
                    eng.wait_ge(sem, val)
                if o["fn"] is None:
                    continue
                ins = o["fn"](eng)
                if attach is not None:
                    ins._wait_ge(attach[0], attach[1])
                if o["dma_sem"] is not None:
                    ins.then_inc(dmasem[o["dma_sem"]], 16)
                elif o["signal"] and ename != "sp":
                    ins.then_inc(engsem[ename], 1)

        with nc.Block() as block:
            @block.tensor
            def _(e):
                run("pe", e)

            @block.scalar
            def _(e):
                run("act", e)

            @block.vector
            def _(e):
                run("dve", e)

            @block.gpsimd
            def _(e):
                run("pool", e)

            @block.sync
            def _(e):
                run("sp", e)


class Tile:
    def __init__(self, nc, name, shape, dtype):
        self.t = nc.alloc_sbuf_tensor(name, list(shape), dtype)
        self.name = name
        self.bufs = {}

    def buf(self, *key):
        b = self.bufs.get(key)
        if b is None:
            b = Buf(f"{self.name}{key}")
            self.bufs[key] = b
        return b


class Ring:
    def __init__(self, nc, name, n, shape, dtype):
        self.tiles = [Tile(nc, f"{name}{i}", shape, dtype) for i in range(n)]
        self.i = 0

    def next(self):
        t = self.tiles[self.i % len(self.tiles)]
        self.i += 1
        return t


def _fm(v):
    v = np.asarray(v, np.float32).reshape(-1, 128)
    return np.ascontiguousarray(v.T)


def _smallp_layout():
    cols = {}
    off = 0

    def add(name, n):
        nonlocal off
        cols[name] = off
        off += n

    add("c", 8)
    for l in range(DEPTH):
        for i in range(4):
            add(f"ng{l}_{i}", 8)
    for l in range(DEPTH):
        add(f"adab{l}", 48)
    for l in range(N_A):
        for t in range(3):
            add(f"cw{l}_{t}", 8)
        add(f"cb{l}", 8)
    add("kvadab", 16)
    add("kvng", 8)
    add("kvang", 2)
    for j in range(DEPTH - N_A):
        add(f"qang{j}", 3)
    return cols, off


SP_COLS, SP_N = _smallp_layout()


def _rope_tables():
    inv_freq = 1.0 / (10000.0 ** (np.arange(0, QK_ROPE, 2, dtype=np.float32) / QK_ROPE))
    ang = np.arange(S, dtype=np.float32)[:, None] * inv_freq[None, :]
    cos = np.cos(ang).astype(np.float32).T
    sin = np.sin(ang).astype(np.float32).T
    cos128 = np.concatenate([cos, cos, cos, cos], 0)
    sins = np.concatenate([-sin, sin, -sin, sin], 0)
    return np.ascontiguousarray(np.stack([cos128, sins], 0))


def build(n_layers=DEPTH, n_groups=NG, dbg=None):
    nc = bass.Bass("TRN2", target_bir_lowering=False)
    P = Prog()

    def din(name, shape):
        return nc.dram_tensor(name, list(shape), F32, kind="ExternalInput").ap()

    xT = din("xT", [D, S])
    smallp_d = din("smallp", [128, SP_N])
    rope_d = din("rope", [2, 128, S])
    ident_d = din("ident", [128, 128])
    ada_w = din("ada_w", [DEPTH, D, 6 * D])
    kv_ada_w = din("kv_ada_w", [D, 2 * D])
    conv_in_r = din("conv_in_r", [N_A, D, 3 * D])
    conv_out_w = din("conv_out_w", [N_A, D, D])
    kv_a_ext = din("kv_a_ext", [D, 512])
    kv_b_w = din("kv_b_w", [KV_LORA, 2048])
    q_a_w = din("q_a_w", [2, D, Q_LORA])
    q_b_ext = din("q_b_ext", [2, Q_LORA, 2048])
    attn_o_w = din("attn_o_w", [2, D, D])
    mlp_up_w = din("mlp_up_w", [DEPTH, D, DFF])
    mlp_down_w = din("mlp_down_w", [DEPTH, DFF, D])
    outT = nc.dram_tensor("outT", [D, S], F32, kind="ExternalOutput").ap()

    X = Tile(nc, "X", [128, DC, GT], F32)
    HB = Tile(nc, "HB", [128, DC, GT], BF16)
    Y = Tile(nc, "Y", [128, DC, GT], F32)
    A = Tile(nc, "A", [128, DFF // 128, GT], BF16)
    WR = [Tile(nc, f"WR{i}", [128, WSLOT], BF16) for i in range(NSLOT)]
    SQ = Ring(nc, "SQ", 3, [128, GT], BF16)
    RS = Ring(nc, "RS", 2, [128, GT], F32)
    TMP = Ring(nc, "TMP", 3, [128, GT], F32)
    UU = Ring(nc, "UU", 2, [128, GT + 2], F32)
    CKV = Tile(nc, "CKV", [128, 2, S], BF16)
    KPE = Tile(nc, "KPE", [128, S], BF16)
    ROPE = Tile(nc, "ROPE", [128, 2, GT], F32)
    QLB = Tile(nc, "QLB", [128, 3, GT], BF16)
    QNR = [Tile(nc, f"QN{i}", [128, 2, GT], BF16) for i in range(2)]
    QRBDR = [Tile(nc, f"QRBD{i}", [128, GT // 128, 256], BF16) for i in range(2)]
    KT = Tile(nc, "KT", [128, 2, S], BF16)
    VA = Tile(nc, "VA", [128, S // 128, 2, 130], BF16)
    PT = Ring(nc, "PT", 3, [128, 512], BF16)
    PTD = Ring(nc, "PTD", 2, [128, 256], BF16)
    OB = Ring(nc, "OB", 4, [128, 128], BF16)
    RINV = Ring(nc, "RINV", 4, [128, 1], F32)
    ADAST = Ring(nc, "ADAST", 4, [128, 1024], F32)
    ACCR = Ring(nc, "ACC", 2, [128, 2048], F32)
    SMP = Tile(nc, "SMP", [128, SP_N], F32)
    CACT = Tile(nc, "CACT", [128, 8], F32)
    MOD = Tile(nc, "MOD", [128, DEPTH, 48], F32)
    KVMOD = Tile(nc, "KVMOD", [128, 16], F32)
    DER = Tile(nc, "DER", [128, DEPTH, 4, 8], F32)
    DKV = Tile(nc, "DKV", [128, 16], F32)
    HALO = Tile(nc, "HALO", [128, N_A, DC, 2], F32)
    ONESB = Tile(nc, "ONESB", [128, 128], BF16)
    ONESF = Tile(nc, "ONESF", [128, 1], F32)
    IDB = Tile(nc, "IDB", [128, 128], BF16)
    FILL = Tile(nc, "FILL", [128, 512], BF16)

    NPSM = 3
    PSM = [nc.alloc_psum_tensor(f"psm{i}", [128, 512], F32) for i in range(NPSM)]
    PSMb = [Buf(f"psm{i}") for i in range(NPSM)]
    NPSN = 1
    PSN = [nc.alloc_psum_tensor(f"psn{i}", [128, 512], F32) for i in range(NPSN)]
    PSNb = [Buf(f"psn{i}") for i in range(NPSN)]
    PST = [nc.alloc_psum_tensor(f"pst{i}", [128, 512], F32) for i in range(2)]
    PSTb = [Buf(f"pst{i}") for i in range(2)]
    PSOT = [nc.alloc_psum_tensor(f"pso{i}", [128, 512], F32) for i in range(2)]
    PSOb = [Buf(f"pso{i}") for i in range(2)]
    PSTR = [[PSOT[i][:, 264 + 64 * h:264 + 64 * (h + 1)].bitcast(BF16) for h in range(2)]
            for i in range(2)]
    PSA = PSOT[0][:, 400:464]
    PSAb = PSOb[0]
    cnt = {"psm": 0, "psn": 0, "pst": 0, "pso": 0, "pstr": 0}

    def psm_next():
        i = cnt["psm"] % NPSM
        cnt["psm"] += 1
        return PSM[i], PSMb[i]

    def mm(out, lhsT, rhs, start, stop, reads, writes, signal=None, sgc=False):
        if sgc:
            P.emit("pe", lambda e: e.matmul(out, lhsT, rhs, start=start, stop=stop,
                                            skip_group_check=True),
                   reads, writes, signal=(stop if signal is None else signal))
        else:
            P.emit("pe", lambda e: e.matmul(out, lhsT, rhs, start=start, stop=stop),
                   reads, writes, signal=(stop if signal is None else signal))

    def act(out, in_, func, reads, writes, bias=None, scale=None):
        kw = {}
        if bias is not None:
            kw["bias"] = bias
        if scale is not None:
            kw["scale"] = scale
        P.emit("act", lambda e: e.activation(out, in_, func, **kw), reads, writes)

    def tt(eng, out, in0, in1, op, reads, writes):
        P.emit(eng, lambda e: e.tensor_tensor(out, in0, in1, op), reads, writes)

    def ts(eng, out, in0, s1, s2, op0, op1, reads, writes):
        if op1 is None:
            P.emit(eng, lambda e: e.tensor_scalar(out, in0, s1, None, op0), reads, writes)
        else:
            P.emit(eng, lambda e: e.tensor_scalar(out, in0, s1, s2, op0, op1), reads, writes)

    def stt(eng, out, in0, scalar, in1, op0, op1, reads, writes):
        P.emit(eng, lambda e: e.scalar_tensor_tensor(out, in0, scalar, in1, op0, op1),
               reads, writes)

    def dma(eng, out, in_, reads, writes, sem):
        P.emit(eng, lambda e: e.dma_start(out=out, in_=in_), reads, writes, dma_sem=sem)

    def memset(eng, ap, val, writes):
        P.emit(eng, lambda e: e.memset(ap, val), (), writes)

    def spc(name, n=8, off=0):
        c0 = SP_COLS[name] + off
        return SMP.t[:, c0:c0 + n]

    SMPb = SMP.buf()

    def wsrc3(w2d, kc, c0, c1):
        return w2d.rearrange("(kc p) n -> p kc n", p=128)[:, 0:kc, c0:c1]

    def weight_schedule():
        for g in range(n_groups):
            for l in range(n_layers):
                if l < N_A:
                    for c in range(DC):
                        yield (f"cin{l}_{c}", conv_in_r[l], 8, c * 384, (c + 1) * 384, 1)
                    for i in range(2):
                        yield (f"cout{l}_{i}", conv_out_w[l], 8, i * 512, (i + 1) * 512, 1)
                else:
                    j = l - N_A
                    if l == N_A:
                        yield ("kva", kv_a_ext, 8, 0, 512, 1)
                    yield (f"kvb{j}", kv_b_w, 2, 0, 2048, 1)
                    yield (f"qa{j}", q_a_w[j], 8, 0, 384, 1)
                    for i in range(2):
                        yield (f"qb{j}_{i}", q_b_ext[j], 3, i * 1024, (i + 1) * 1024, 1)
                    for i in range(2):
                        yield (f"ow{j}_{i}", attn_o_w[j], 8, i * 512, (i + 1) * 512, 1)
                for i in range(8):
                    yield (f"up{l}_{i}", mlp_up_w[l], 8, i * 512, (i + 1) * 512, 1)
                for i in range(8):
                    yield (f"dn{l}_{i}", mlp_down_w[l], 32, i * 128, (i + 1) * 128, 2)

    wsched = list(weight_schedule())
    wstate = {"next": 0}
    WRb = [WR[i].buf() for i in range(NSLOT)]
    ntg = len(wsched) // n_groups
    use_cache = n_groups > 1
    if use_cache:
        wcache = nc.dram_tensor("wcache", [ntg, 128, WSLOT], BF16).ap()
        SCRb = [Buf(f"scr{t}") for t in range(ntg)]

    def w_issue(i):
        if i >= len(wsched):
            return
        name, w2d, kc, c0, c1, nsplit = wsched[i]
        slot = i % NSLOT
        ncols = c1 - c0
        if use_cache and i >= ntg:
            t = i % ntg
            dma("pool", WR[slot].t[:, 0:kc * ncols], wcache[t][:, 0:kc * ncols], [SCRb[t]],
                [WRb[slot]], f"w{slot}_{(i // NSLOT) % 3}")
            return
        dst = WR[slot].t[:, 0:kc * ncols].rearrange("p (k n) -> p k n", k=kc)
        src = wsrc3(w2d, kc, c0, c1)
        ks = kc // nsplit
        for s in range(nsplit):
            dma("pool", dst[:, s * ks:(s + 1) * ks, :], src[:, s * ks:(s + 1) * ks, :],
                (), [WRb[slot]], f"w{slot}_{(i // NSLOT) % 3}")

    def w_writeback(i):
        if not use_cache or i >= ntg:
            return
        name, w2d, kc, c0, c1, nsplit = wsched[i]
        slot = i % NSLOT
        n = kc * (c1 - c0)
        dma("sp", wcache[i][:, 0:n], WR[slot].t[:, 0:n], [WRb[slot]], [SCRb[i]], f"ws{slot}")

    def w_acquire(name):
        i = wstate["next"]
        wstate["next"] += 1
        spec = wsched[i]
        assert spec[0] == name, (spec[0], name)
        kc, ncols = spec[2], spec[4] - spec[3]
        slot = i % NSLOT
        view = WR[slot].t[:, 0:kc * ncols].rearrange("p (k n) -> p k n", k=kc)
        return i, view, WRb[slot]

    def w_release(i):
        w_writeback(i)
        w_issue(i + NSLOT)
        pump(2)

    dma("sp", SMP.t[:, :], smallp_d, (), [SMPb], "smp")
    memset("dve", ONESB.t[:, :], 1.0, [ONESB.buf()])
    memset("dve", ONESF.t[:, :], 1.0, [ONESF.buf()])
    memset("dve", VA.t[:, :, :, :], 1.0, [VA.buf()])
    for r in PTD.tiles:
        memset("dve", r.t[:, :], 0.0, [r.buf()])
    for r in QRBDR:
        memset("dve", r.t[:, :, :], 0.0, [r.buf()])
    memset("dve", FILL.t[:, :], 0.0, [FILL.buf()])
    memset("dve", HALO.t[:, :, :, :], 0.0, [HALO.buf(l, c) for l in range(N_A) for c in range(DC)])
    dma("pool", IDB.t[:, :], ident_d, (), [IDB.buf()], "ident")
    for i in range(NSLOT):
        w_issue(i)
    act(CACT.t[:, :], spc("c"), AF.Silu, [SMPb], [CACT.buf()])

    micro = []

    def ada_part(w2d, col0, bias_cols, out_ap, out_bufs):
        st_ = {}

        def kstep(kc):
            if kc == 0:
                st_["acc"] = ACCR.next()
            acc = st_["acc"]
            for hf in range(2):
                st = ADAST.next()
                c0 = col0 + hf * 1024
                dma("sp", st.t[:, :], w2d[kc * 128:(kc + 1) * 128, c0:c0 + 1024], (),
                    [st.buf()], f"ada{(ADAST.i - 1) % 4}")
                av = acc.t[:, hf * 1024:(hf + 1) * 1024]
                if kc == 0:
                    ts("dve", av, st.t[:, :], CACT.t[:, 0:1], None, ALU.mult, None,
                       [st.buf(), CACT.buf()], [acc.buf(hf)])
                else:
                    stt("dve", av, st.t[:, :], CACT.t[:, kc:kc + 1], av, ALU.mult, ALU.add,
                        [st.buf(), CACT.buf(), acc.buf(hf)], [acc.buf(hf)])

        def red():
            acc = st_["acc"]
            for i in range(16):
                mm(PSA[:, i:i + 1], acc.t[:, i * 128:(i + 1) * 128], ONESF.t[:, 0:1], True, True,
                   [acc.buf(i // 8), ONESF.buf()], [PSAb], signal=(i == 15))
            tt("dve", out_ap, PSA[:, 0:16], bias_cols, ALU.add, [PSAb, SMPb], out_bufs)

        return [(lambda kc=kc: kstep(kc)) for kc in range(8)] + [red]

    def schedule_parts(parts, tail):
        out = []
        prev = None
        for p_ in parts:
            out += p_[0:8]
            if prev is not None:
                out.append(prev)
            prev = p_[8]
        out.append(prev)
        return out + tail

    def ada_layer_parts(l):
        return [ada_part(ada_w[l], j * 1024, spc(f"adab{l}", 16, j * 8),
                         MOD.t[:, l, j * 8:(j + 2) * 8], [MOD.buf(l, j), MOD.buf(l, j + 1)])
                for j in (0, 2, 4)]

    def kv_parts():
        return [ada_part(kv_ada_w, 0, spc("kvadab", 16, 0), KVMOD.t[:, 0:16],
                         [KVMOD.buf(0), KVMOD.buf(1)])]

    def pump(k):
        for _ in range(k):
            if micro:
                micro.pop(0)()

    derived = set()

    def derive_k(l, k):
        if k in (0, 2):
            jsc, gi = (1, 0) if k == 0 else (4, 2)
            ts("dve", DER.t[:, l, k, :], MOD.t[:, l, jsc * 8:(jsc + 1) * 8], 1.0, 32.0, ALU.add,
               ALU.mult, [MOD.buf(l, jsc)], [DER.buf(l, k)])
            tt("dve", DER.t[:, l, k, :], DER.t[:, l, k, :], spc(f"ng{l}_{gi}"), ALU.mult,
               [DER.buf(l, k), SMPb], [DER.buf(l, k)])
        else:
            jg, gi = (2, 1) if k == 1 else (5, 3)
            stt("dve", DER.t[:, l, k, :], MOD.t[:, l, jg * 8:(jg + 1) * 8], 32.0,
                spc(f"ng{l}_{gi}"), ALU.mult, ALU.mult, [MOD.buf(l, jg), SMPb], [DER.buf(l, k)])
        derived.add((l, k))

    def derive_layer(l):
        for k in range(4):
            derive_k(l, k)

    def ensure(l, k):
        while (l, k) not in derived:
            assert micro, ("mods not scheduled", l, k)
            micro.pop(0)()

    def derive_kv():
        ts("dve", DKV.t[:, 0:8], KVMOD.t[:, 8:16], 1.0, 32.0, ALU.add, ALU.mult,
           [KVMOD.buf(1)], [DKV.buf(0)])
        tt("dve", DKV.t[:, 0:8], DKV.t[:, 0:8], spc("kvng"), ALU.mult, [DKV.buf(0), SMPb],
           [DKV.buf(0)])
        ts("dve", DKV.t[:, 8:10], spc("kvang", 2), 16.0, None, ALU.mult, None, [SMPb],
           [DKV.buf(1)])
        for j in range(2):
            ts("dve", DKV.t[:, 10 + 3 * j:13 + 3 * j], spc(f"qang{j}", 3),
               float(np.sqrt(Q_LORA)) * SM_SCALE, None, ALU.mult, None, [SMPb], [DKV.buf(2 + j)])

    def sq_accum(src_ap, src_bufs, idx, total, kparts=128):
        sq = SQ.next()
        act(sq.t[0:kparts, :], src_ap, AF.Square, src_bufs, [sq.buf()])
        i = cnt["psn"] % NPSN
        mm(PSN[i][:, :], ONESB.t[0:kparts, :], sq.t[0:kparts, :], idx == 0, idx == total - 1,
           [ONESB.buf(), sq.buf()], [PSNb[i]], signal=True)

    def pe_fill(n):
        if n <= 0:
            return
        ps, psb = psm_next()
        for i in range(n):
            mm(ps[:, :], ONESB.t[:, :], FILL.t[:, :], True, True, [ONESB.buf(), FILL.buf()],
               [psb], signal=(i == n - 1))

    def rs_finish(n_eps):
        i = cnt["psn"] % NPSN
        cnt["psn"] += 1
        rs = RS.next()
        act(rs.t[:, :], PSN[i][:, :], AF.Sqrt, [PSNb[i]], [rs.buf()], bias=float(n_eps))
        P.emit("dve", lambda e: e.reciprocal(rs.t[:, :], rs.t[:, :]), [rs.buf()], [rs.buf()])
        return rs

    def norm_mod(scale_ap_fn, shift_ap_fn, sbufs, rs=None):
        if rs is None:
            pe_fill(NFILL)
            for c in range(DC):
                sq_accum(X.t[:, c, :], [X.buf(c)], c, DC)
            rs = rs_finish(D * EPS)
        for c in range(DC):
            tm = TMP.next()
            tt("dve", tm.t[:, :], X.t[:, c, :], rs.t[:, :], ALU.mult, [X.buf(c), rs.buf()],
               [tm.buf()])
            act(HB.t[:, c, :], tm.t[:, :], AF.Identity, [tm.buf()] + sbufs, [HB.buf(c)],
                bias=shift_ap_fn(c), scale=scale_ap_fn(c))
        return rs

    def resid(l, k):
        rs = rs_finish(D * EPS)
        prev = None
        for c in range(DC + 1):
            cur = None
            if c < DC:
                tm = TMP.next()
                tt("dve", tm.t[:, :], Y.t[:, c, :], rs.t[:, :], ALU.mult, [Y.buf(c), rs.buf()],
                   [tm.buf()])
                cur = (c, tm)
            if prev is not None:
                pc, ptm = prev
                tt("dve", X.t[:, pc, :], X.t[:, pc, :], ptm.t[:, :], ALU.add,
                   [ptm.buf(), X.buf(pc)], [X.buf(pc)])
            prev = cur

    def y_evac(l, k, n, ps, psb):
        ensure(l, k)
        sq_accum(ps[:, :], [psb], n, DC)
        act(Y.t[:, n, :], ps[:, :], AF.Identity, [psb, DER.buf(l, k)], [Y.buf(n)],
            scale=DER.t[:, l, k, n:n + 1])

    def linear(in_fn, in_bufs_fn, KC, tiles, consumer, kparts=128, stream_in=False):
        n = 0
        tiles = list(tiles)
        done = {}
        if stream_in:
            heads = []
            held = []
            tpos = 0
            while len(heads) < NPSM and tpos < len(tiles):
                tname, nper = tiles[tpos]
                ti, wv, wb = w_acquire(tname)
                take = min(nper, NPSM - len(heads))
                for nn in range(take):
                    heads.append((wv, wb, nn))
                held.append((tpos, ti, wv, wb, take))
                tpos += 1
            pss = [psm_next() for _ in heads]
            for kc in range(KC):
                for hi_, (wv, wb, nn) in enumerate(heads):
                    ps, psb = pss[hi_]
                    mm(ps[:, :], wv[0:kparts, kc, nn * 128:(nn + 1) * 128], in_fn(kc),
                       kc == 0, kc == KC - 1, [wb] + in_bufs_fn(kc), [psb], signal=True)
            for hi_ in range(len(heads)):
                consumer(n, pss[hi_][0], pss[hi_][1])
                n += 1
            for (tp, ti, wv, wb, take) in held:
                done[tp] = (ti, wv, wb, take)
        for tpos, (tname, nper) in enumerate(tiles):
            if tpos in done:
                ti, wv, wb, nn0 = done[tpos]
            else:
                ti, wv, wb = w_acquire(tname)
                nn0 = 0
            for nn in range(nn0, nper):
                ps, psb = psm_next()
                for kc in range(KC):
                    mm(ps[:, :], wv[0:kparts, kc, nn * 128:(nn + 1) * 128], in_fn(kc),
                       kc == 0, kc == KC - 1, [wb] + in_bufs_fn(kc), [psb])
                consumer(n, ps, psb)
                n += 1
            w_release(ti)

    def conv_mixer(l, g):
        ensure(l, 0)
        def in_fn(kc):
            return HB.t[:, kc, :]

        def in_bufs(kc):
            return [HB.buf(kc)]

        norm_mod(lambda c: DER.t[:, l, 0, c:c + 1], lambda c: MOD.t[:, l, 0 + c:c + 1],
                 [DER.buf(l, 0), MOD.buf(l, 0)])
        for c in range(DC):
            ti, wv, wb = w_acquire(f"cin{l}_{c}")
            pss = []
            if c == 0:
                pss = [psm_next() for _ in range(3)]
                for kc in range(DC):
                    for part in range(3):
                        mm(pss[part][0][:, :], wv[:, kc, part * 128:(part + 1) * 128],
                           HB.t[:, kc, :], kc == 0, kc == DC - 1, [wb, HB.buf(kc)],
                           [pss[part][1]], signal=True)
            for part in range(3):
                if c == 0:
                    ps, psb = pss[part]
                else:
                    ps, psb = psm_next()
                    for kc in range(DC):
                        mm(ps[:, :], wv[:, kc, part * 128:(part + 1) * 128], HB.t[:, kc, :],
                           kc == 0, kc == DC - 1, [wb, HB.buf(kc)], [psb])
                if part == 0:
                    cg = TMP.next()
                    act(cg.t[:, :], ps[:, :], AF.Identity, [psb], [cg.buf()])
                elif part == 1:
                    u = UU.next()
                    act(u.t[:, 0:2], HALO.t[:, l, c, :], AF.Identity, [HALO.buf(l, c)], [u.buf()])
                    tt("dve", u.t[:, 2:GT + 2], cg.t[:, :], ps[:, :], ALU.mult, [cg.buf(), psb],
                       [u.buf()])
                    act(HALO.t[:, l, c, :], u.t[:, GT:GT + 2], AF.Identity, [u.buf()],
                        [HALO.buf(l, c)])
                    cv = TMP.next()
                    ts("dve", cv.t[:, :], u.t[:, 2:GT + 2], spc(f"cw{l}_2", 1, c),
                       spc(f"cb{l}", 1, c), ALU.mult, ALU.add, [u.buf(), SMPb], [cv.buf()])
                    stt("dve", cv.t[:, :], u.t[:, 1:GT + 1], spc(f"cw{l}_1", 1, c), cv.t[:, :],
                        ALU.mult, ALU.add, [u.buf(), SMPb, cv.buf()], [cv.buf()])
                    stt("dve", cv.t[:, :], u.t[:, 0:GT], spc(f"cw{l}_0", 1, c), cv.t[:, :],
                        ALU.mult, ALU.add, [u.buf(), SMPb, cv.buf()], [cv.buf()])
                else:
                    tt("dve", A.t[:, c, :], ps[:, :], cv.t[:, :], ALU.mult, [psb, cv.buf()],
                       [A.buf(c)])
            w_release(ti)

        linear(lambda kc: A.t[:, kc, :], lambda kc: [A.buf(kc)], DC,
               [(f"cout{l}_0", 4), (f"cout{l}_1", 4)],
               lambda n, ps, psb: y_evac(l, 1, n, ps, psb), stream_in=True)
        resid(l, 1)

    def mlp(l, g):
        ensure(l, 2)
        ensure(l, 3)
        norm_mod(lambda c: DER.t[:, l, 2, c:c + 1], lambda c: MOD.t[:, l, 24 + c:25 + c],
                 [DER.buf(l, 2), MOD.buf(l, 3)])

        def cons_up(n, ps, psb):
            r = TMP.next()
            act(r.t[:, :], ps[:, :], AF.Relu, [psb], [r.buf()])
            act(A.t[:, n, :], r.t[:, :], AF.Square, [r.buf()], [A.buf(n)])

        linear(lambda kc: HB.t[:, kc, :], lambda kc: [HB.buf(kc)], DC,
               [(f"up{l}_{i}", 4) for i in range(8)], cons_up, stream_in=True)

        linear(lambda kc: A.t[:, kc, :], lambda kc: [A.buf(kc)], DFF // 128,
               [(f"dn{l}_{i}", 1) for i in range(8)],
               lambda n, ps, psb: y_evac(l, 3, n, ps, psb))
        resid(l, 3)

    def shared_kv(g, rs):
        t0 = g * GT
        norm_mod(lambda c: DKV.t[:, c:c + 1], lambda c: KVMOD.t[:, c:c + 1],
                 [DKV.buf(0), KVMOD.buf(0)], rs=rs)
        lat = {}

        def cons(n, ps, psb):
            if n < 2:
                act(Y.t[:, n, :], ps[:, :], AF.Identity, [psb], [Y.buf(n)])
                sq_accum(ps[:, :], [psb], n, 2)
                lat[n] = True
            elif n == 2:
                tm = TMP.next()
                tt("dve", tm.t[:, :], ps[:, :], ROPE.t[:, 0, :], ALU.mult, [psb, ROPE.buf()],
                   [tm.buf()])
                lat["pe"] = tm
            else:
                tm2 = TMP.next()
                tt("dve", tm2.t[:, :], ps[:, :], ROPE.t[:, 1, :], ALU.mult, [psb, ROPE.buf()],
                   [tm2.buf()])
                tm = lat["pe"]
                tt("dve", KPE.t[:, t0:t0 + GT], tm.t[:, :], tm2.t[:, :], ALU.add,
                   [tm.buf(), tm2.buf()], [KPE.buf(g)])

        linear(lambda kc: HB.t[:, kc, :], lambda kc: [HB.buf(kc)], DC, [("kva", 4)], cons,
               stream_in=True)
        rsk = rs_finish(KV_LORA * EPS)
        for n in range(2):
            tm = TMP.next()
            tt("dve", tm.t[:, :], Y.t[:, n, :], rsk.t[:, :], ALU.mult, [Y.buf(n), rsk.buf()],
               [tm.buf()])
            act(CKV.t[:, n, t0:t0 + GT], tm.t[:, :], AF.Identity, [tm.buf(), DKV.buf(1)],
                [CKV.buf(n, g)], scale=DKV.t[:, 8 + n:9 + n])

    def attention(l, g, rs_in):
        j = l - N_A
        t0 = g * GT
        kend = t0 + GT
        nkc = kend // 128
        tkv, wkv, wkvb = w_acquire(f"kvb{j}")

        def kv_pair(hp):
            for hi in range(2):
                h = 2 * hp + hi
                for kg in range(kend // 512):
                    ps, psb = psm_next()
                    for kc in range(2):
                        mm(ps[:, :], wkv[:, kc, h * 256:h * 256 + 128],
                           CKV.t[:, kc, kg * 512:(kg + 1) * 512], kc == 0, kc == 1,
                           [wkvb, CKV.buf(kc, kg)], [psb])
                    act(KT.t[:, hi, kg * 512:(kg + 1) * 512], ps[:, :], AF.Identity, [psb],
                        [KT.buf(hi, kg)])
            for tc4 in range(nkc // 2):
                ps, psb = psm_next()
                for sub in range(2):
                    tcn = tc4 * 2 + sub
                    for hi in range(2):
                        h = 2 * hp + hi
                        for kc in range(2):
                            mm(ps[:, sub * 256 + hi * 128:sub * 256 + (hi + 1) * 128],
                               CKV.t[:, kc, tcn * 128:(tcn + 1) * 128],
                               wkv[:, kc, h * 256 + 128:h * 256 + 256], kc == 0, kc == 1,
                               [wkvb, CKV.buf(kc, tcn // 4)], [psb],
                               signal=(kc == 1 and hi == 1 and sub == 1))
                for sub in range(2):
                    tcn = tc4 * 2 + sub
                    P.emit("dve", lambda e, tcn=tcn, sub=sub, ps=ps: e.tensor_copy(
                        VA.t[:, tcn, :, 0:128],
                        ps[:, sub * 256:(sub + 1) * 256].rearrange("p (h d) -> p h d", h=2)),
                        [psb], [VA.buf(tcn)])

        kv_pair(0)
        ensure(l, 0)
        norm_mod(lambda c: DER.t[:, l, 0, c:c + 1], lambda c: MOD.t[:, l, 0 + c:c + 1],
                 [DER.buf(l, 0), MOD.buf(l, 0)], rs=rs_in)

        def cons_qa(n, ps, psb):
            act(Y.t[:, n, :], ps[:, :], AF.Identity, [psb], [Y.buf(n)])
            sq_accum(ps[:, :], [psb], n, 3)

        linear(lambda kc: HB.t[:, kc, :], lambda kc: [HB.buf(kc)], DC, [(f"qa{j}", 3)], cons_qa,
               stream_in=True)
        rsq = rs_finish(Q_LORA * EPS)
        for n in range(3):
            tm = TMP.next()
            tt("dve", tm.t[:, :], Y.t[:, n, :], rsq.t[:, :], ALU.mult, [Y.buf(n), rsq.buf()],
               [tm.buf()])
            act(QLB.t[:, n, :], tm.t[:, :], AF.Identity, [tm.buf(), DKV.buf(2 + j)], [QLB.buf(n)],
                scale=DKV.t[:, 10 + 3 * j + n:11 + 3 * j + n])
        qst = {"tq": None}

        def q_proj(hp):
            QN = QNR[hp % 2]
            QRBD = QRBDR[hp % 2]
            if hp % 2 == 0:
                qst["tq"] = w_acquire(f"qb{j}_{hp // 2}")
            tq, wq, wqb = qst["tq"]
            cb = (hp % 2) * 512
            for part in range(4):
                ps, psb = psm_next()
                for kc in range(3):
                    mm(ps[:, :], wq[:, kc, cb + part * 128:cb + (part + 1) * 128],
                       QLB.t[:, kc, :], kc == 0, kc == 2, [wqb, QLB.buf(kc)], [psb])
                if part < 2:
                    act(QN.t[:, part, :], ps[:, :], AF.Identity, [psb], [QN.buf(part)])
                elif part == 2:
                    tpe = TMP.next()
                    tt("dve", tpe.t[:, :], ps[:, :], ROPE.t[:, 0, :], ALU.mult,
                       [psb, ROPE.buf()], [tpe.buf()])
                else:
                    tsw = TMP.next()
                    tt("dve", tsw.t[:, :], ps[:, :], ROPE.t[:, 1, :], ALU.mult,
                       [psb, ROPE.buf()], [tsw.buf()])
                    for hh in range(2):
                        r0 = hh * 64
                        tt("dve", QRBD.t[r0:r0 + 64, :, hh * 128:(hh + 1) * 128],
                           tpe.t[r0:r0 + 64, :].rearrange("p (q t) -> p q t", t=128),
                           tsw.t[r0:r0 + 64, :].rearrange("p (q t) -> p q t", t=128), ALU.add,
                           [tpe.buf(), tsw.buf()], [QRBD.buf()])
            if hp % 2 == 1:
                w_release(tq)

        q_proj(0)
        for hp in range(4):
            if hp < 3:
                q_proj(hp + 1)
            QN = QNR[hp % 2]
            QRBD = QRBDR[hp % 2]
            items = []
            for qb in range(GT // 128):
                qbg = t0 // 128 + qb
                kcs = list(range(qbg))
                bts = [kcs[b0:b0 + 2] for b0 in range(0, len(kcs), 2)] + [[qbg]]
                step = 0
                for bi_, bt in enumerate(bts):
                    items.append(dict(qb=qb, qbg=qbg, bt=bt, diag=(bt[0] == qbg), step0=step,
                                      nsteps=qbg + 1, last=(bi_ == len(bts) - 1),
                                      first=(bi_ == 0)))
                    step += len(bt)
            psomap = {}

            def emit_st(it):
                qb, bt = it["qb"], it["bt"]
                si = cnt["pst"] % 2
                cnt["pst"] += 1
                pst, pstb = PST[si], PSTb[si]
                it["pst"] = (pst, pstb)
                for bi, kc in enumerate(bt):
                    c0 = bi * 256
                    mm(pst[:, c0:c0 + 256], KPE.t[:, kc * 128:(kc + 1) * 128], QRBD.t[:, qb, :],
                       True, False, [KPE.buf(kc // 4), QRBD.buf()], [pstb], signal=False,
                       sgc=True)
                    for hi in range(2):
                        mm(pst[:, c0 + hi * 128:c0 + (hi + 1) * 128],
                           KT.t[:, hi, kc * 128:(kc + 1) * 128],
                           QN.t[:, hi, qb * 128:(qb + 1) * 128], False, hi == 1,
                           [KT.buf(hi, kc // 4), QN.buf(hi)], [pstb],
                           signal=(hi == 1 and bi == len(bt) - 1), sgc=True)

            def emit_exp(it):
                pst, pstb = it["pst"]
                if it["diag"]:
                    pt = PTD.next()
                    act(pt.t[0:64, :], pst[0:64, 0:256], AF.Exp, [pstb], [pt.buf()])
                    ov = pt.t[64:128, :].rearrange("p (h c j) -> p h c j", h=2, c=2)[:, :, 1, :]
                    iv = pst[64:128, 0:256].rearrange("p (h c j) -> p h c j", h=2, c=2)[:, :, 1, :]
                    act(ov, iv, AF.Exp, [pstb], [pt.buf()])
                else:
                    pt = PT.next()
                    w = 256 * len(it["bt"])
                    act(pt.t[:, 0:w], pst[:, 0:w], AF.Exp, [pstb], [pt.buf()])
                it["pt"] = pt

            def emit_pv(it):
                qb, bt = it["qb"], it["bt"]
                if it["first"]:
                    oi = cnt["pso"] % 2
                    cnt["pso"] += 1
                    psomap[qb] = oi
                oi = psomap[qb]
                psot, psob = PSOT[oi], PSOb[oi]
                pt = it["pt"]
                step = it["step0"]
                for bi, kc in enumerate(bt):
                    for hi in range(2):
                        mm(psot[:, hi * 132:hi * 132 + 129],
                           pt.t[:, bi * 256 + hi * 128:bi * 256 + (hi + 1) * 128],
                           VA.t[:, kc, hi, 0:129], step == 0 and hi == 0,
                           step == it["nsteps"] - 1, [pt.buf(), VA.buf(kc)], [psob],
                           signal=(bi == len(bt) - 1 and hi == 1), sgc=True)
                    step += 1

            def emit_fin_a(it):
                oi = psomap[it["qb"]]
                psot, psob = PSOT[oi], PSOb[oi]
                it["oi"] = oi
                obs = []
                rinvs = []
                for hi in range(2):
                    rinv = RINV.next()
                    P.emit("dve", lambda e, rinv=rinv, psot=psot, hi=hi: e.reciprocal(
                        rinv.t[:, :], psot[:, hi * 132 + 128:hi * 132 + 129]), [psob],
                        [rinv.buf()])
                    rinvs.append(rinv)
                for hi in range(2):
                    rinv = rinvs[hi]
                    ob = OB.next()
                    ts("dve", ob.t[:, :], psot[:, hi * 132:hi * 132 + 128], rinv.t[:, 0:1], None,
                       ALU.mult, None, [psob, rinv.buf()], [ob.buf()])
                    obs.append(ob)
                it["obs"] = obs

            def emit_fin_b(it):
                oi = it["oi"]
                qb = it["qb"]
                for hi in range(2):
                    ob = it["obs"][hi]
                    h = 2 * hp + hi
                    trp = PSTR[oi][hi]
                    P.emit("pe", lambda e, trp=trp, ob=ob: e.transpose(trp, ob.t[:, :],
                                                                       IDB.t[:, :]),
                           [ob.buf(), IDB.buf()], [PSOb[oi]])
                    act(A.t[:, h, qb * 128:(qb + 1) * 128], trp, AF.Identity, [PSOb[oi]],
                        [A.buf(h)])

            emit_st(items[0])
            if len(items) > 1:
                emit_st(items[1])
            pend = None
            for i_, it in enumerate(items):
                emit_exp(it)
                if i_ + 2 < len(items):
                    emit_st(items[i_ + 2])
                emit_pv(it)
                if pend is not None:
                    emit_fin_b(pend)
                    pend = None
                if it["last"]:
                    emit_fin_a(it)
                    pend = it
            emit_fin_b(pend)
            if hp < 3:
                kv_pair(hp + 1)
                if hp == 2:
                    w_release(tkv)

        linear(lambda kc: A.t[:, kc, :], lambda kc: [A.buf(kc)], DC,
               [(f"ow{j}_0", 4), (f"ow{j}_1", 4)],
               lambda n, ps, psb: y_evac(l, 1, n, ps, psb), stream_in=True)
        resid(l, 1)

    p0 = ada_layer_parts(0)
    for st_ in p0[0]:
        st_()
    derive_k(0, 0)
    micro.extend(p0[1][0:8] + [p0[1][8], lambda: derive_k(0, 1)] + p0[2][0:8]
                 + [p0[2][8], lambda: derive_k(0, 2), lambda: derive_k(0, 3)])

    for g in range(n_groups):
        t0 = g * GT
        for c in range(DC):
            dma("sp", X.t[:, c, :], xT[c * 128:(c + 1) * 128, t0:t0 + GT], (), [X.buf(c)],
                f"xin{c}")
        if n_layers > N_A:
            dma("sp", ROPE.t[:, :, :], rope_d.rearrange("a p t -> p a t")[:, :, t0:t0 + GT], (),
                [ROPE.buf()], "rope")
        for l in range(n_layers):
            if g == 0 and l + 1 < n_layers:
                nl = l + 1
                parts = ada_layer_parts(nl)
                tail = [lambda nl=nl: derive_layer(nl)]
                if nl == N_A:
                    parts += kv_parts()
                    tail.append(derive_kv)
                micro.extend(schedule_parts(parts, tail))
            if l < N_A:
                conv_mixer(l, g)
            else:
                rs = None
                if l == N_A:
                    pe_fill(NFILL)
                    for c in range(DC):
                        sq_accum(X.t[:, c, :], [X.buf(c)], c, DC)
                    rs = rs_finish(D * EPS)
                    shared_kv(g, rs)
                attention(l, g, rs)
            mlp(l, g)
            pump(10 ** 6)
        for c in range(DC):
            dma("sp", outT[c * 128:(c + 1) * 128, t0:t0 + GT], X.t[:, c, :], [X.buf(c)], [],
                f"xout{c}")
    fins = []
    for c in range(DC):
        fb = Buf(f"fin{c}")
        fb.w = ("d", f"xout{c}", P.dma_val[f"xout{c}"])
        fins.append(fb)
    P.emit("sp", None, fins, [])
    with nc.allow_low_precision("bf16 matmul operands, fp32 accumulation"):
        P.lower(nc)
    return nc


def prep_inputs(inp):
    f = lambda a: np.ascontiguousarray(np.asarray(a, np.float32))
    x = f(inp["x"])
    B = x.shape[0]
    ci = f(inp["conv_in_w"])
    bg, cg, xh = ci[:, :, 0:D], ci[:, :, D:2 * D], ci[:, :, 2 * D:3 * D]
    parts = []
    for c in range(DC):
        sl = slice(c * 128, (c + 1) * 128)
        parts += [cg[:, :, sl], xh[:, :, sl], bg[:, :, sl]]
    conv_in_r = np.ascontiguousarray(np.concatenate(parts, axis=2))
    kva = f(inp["kv_a_w"])
    pe = kva[:, KV_LORA:KV_LORA + 64]
    sw = np.concatenate([pe[:, 32:64], pe[:, 0:32]], axis=1)
    kv_a_ext = np.ascontiguousarray(np.concatenate([kva[:, :KV_LORA], pe, pe, sw, sw], axis=1))
    qb = f(inp["q_b_w"])
    qparts = []
    for hp in range(4):
        h0, h1 = 2 * hp, 2 * hp + 1
        n0 = qb[:, :, h0 * 192:h0 * 192 + 128]
        n1 = qb[:, :, h1 * 192:h1 * 192 + 128]
        p0 = qb[:, :, h0 * 192 + 128:h0 * 192 + 192]
        p1 = qb[:, :, h1 * 192 + 128:h1 * 192 + 192]
        s0 = np.concatenate([p0[:, :, 32:64], p0[:, :, 0:32]], axis=2)
        s1 = np.concatenate([p1[:, :, 32:64], p1[:, :, 0:32]], axis=2)
        qparts += [n0, n1, p0, p1, s0, s1]
    q_b_ext = np.ascontiguousarray(np.concatenate(qparts, axis=2))
    rope = _rope_tables()
    shared = dict(
        rope=rope, ident=np.eye(128, dtype=np.float32), ada_w=f(inp["ada_w"]), kv_ada_w=f(inp["kv_ada_w"]), conv_in_r=conv_in_r,
        conv_out_w=f(inp["conv_out_w"]), kv_a_ext=kv_a_ext, kv_b_w=f(inp["kv_b_w"]),
        q_a_w=f(inp["q_a_w"]), q_b_ext=q_b_ext, attn_o_w=f(inp["attn_o_w"]),
        mlp_up_w=f(inp["mlp_up_w"]), mlp_down_w=f(inp["mlp_down_w"]))
    sm = np.zeros((128, SP_N), np.float32)

    def put(name, v):
        a = _fm(v)
        sm[:, SP_COLS[name]:SP_COLS[name] + a.shape[1]] = a

    ng = f(inp["norm_g"])
    adab = f(inp["ada_b"])
    for l in range(DEPTH):
        for i in range(4):
            put(f"ng{l}_{i}", ng[l, i])
        put(f"adab{l}", adab[l])
    cw = f(inp["conv_w"])
    cbv = f(inp["conv_b"])
    for l in range(N_A):
        for t in range(3):
            put(f"cw{l}_{t}", cw[l, t])
        put(f"cb{l}", cbv[l])
    put("kvadab", f(inp["kv_ada_b"]))
    put("kvng", f(inp["kv_norm_g"]))
    put("kvang", f(inp["kv_a_norm_g"]))
    qg = f(inp["q_a_norm_g"])
    for j in range(DEPTH - N_A):
        put(f"qang{j}", qg[j])
    c = f(inp["c"])
    in_maps = []
    for b in range(B):
        smb = sm.copy()
        smb[:, SP_COLS["c"]:SP_COLS["c"] + 8] = _fm(c[b])
        m = dict(shared)
        m["xT"] = np.ascontiguousarray(x[b].T)
        m["smallp"] = smb
        in_maps.append(m)
    return in_maps


_NC_CACHE = {}


def kernel(**inputs):
    in_maps = prep_inputs(inputs)
    if "nc" not in _NC_CACHE:
        _NC_CACHE["nc"] = build()
    nc = _NC_CACHE["nc"]
    res = run_bass_kernel_spmd(nc, in_maps, core_ids=list(range(len(in_maps))))
    out = np.stack([np.ascontiguousarray(r["outT"].T) for r in res.results], axis=0)
    return out.astype(np.float32)
```

```python
import numpy as np
import concourse.bass as bass
import concourse.mybir as mybir
from concourse.bass_utils import run_bass_kernel_spmd

F32 = mybir.dt.float32
BF16 = mybir.dt.bfloat16
AF = mybir.ActivationFunctionType
ALU = mybir.AluOpType

D = 1024
S = 2048
DEPTH = 4
N_A = 2
DFF = 4096
NH = 8
QK_NOPE = 128
QK_ROPE = 64
V_HEAD = 128
Q_LORA = 384
KV_LORA = 256
EPS = 1e-6
CHUNK = 64
DC = D // 128
GT = 512
NG = S // GT
NSLOT = 4
WSLOT = 4096
SM_SCALE = 1.0 / float(np.sqrt(QK_NOPE + QK_ROPE))
ATTACH_WAIT = True
NFILL = 0


class Buf:
    __slots__ = ("name", "w", "r")

    def __init__(self, name):
        self.name = name
        self.w = None
        self.r = []


class Prog:
    ENGS = ("pe", "act", "dve", "pool", "sp")

    def __init__(self):
        self.ops = {e: [] for e in self.ENGS}
        self.gseq = 0
        self.dma_val = {}

    def emit(self, eng, fn, reads=(), writes=(), signal=True, dma_sem=None):
        idx = len(self.ops[eng])
        if dma_sem is not None:
            v = self.dma_val.get(dma_sem, 0) + 16
            self.dma_val[dma_sem] = v
            tok = ("d", dma_sem, v)
        else:
            tok = ("c", eng, idx)
        deps = set()

        def same(t):
            return dma_sem is None and t[0] == "c" and t[1] == eng

        for b in reads:
            if b.w is not None:
                if not (same(b.w) and eng == "pe"):
                    deps.add(b.w)
        for b in writes:
            for t in b.r:
                if not (same(t) and eng == "pe"):
                    deps.add(t)
            if b.w is not None and not (same(b.w) and eng == "pe"):
                deps.add(b.w)
        for b in reads:
            b.r.append(tok)
        for b in writes:
            b.w = tok
            b.r = []
        self.ops[eng].append(dict(fn=fn, deps=deps, signal=signal, dma_sem=dma_sem,
                                  gseq=self.gseq))
        self.gseq += 1
        return tok

    def lower(self, nc):
        engsem = {e: nc.alloc_semaphore(f"s_{e}") for e in ("pe", "act", "dve", "pool")}
        dmasem = {k: nc.alloc_semaphore(f"d_{k}") for k in self.dma_val}
        cum = {}
        nxt = {}
        for e in ("pe", "act", "dve", "pool"):
            ops = self.ops[e]
            if ops:
                ops[-1]["signal"] = True
            c = 0
            cl = []
            for o in ops:
                if o["signal"] and o["dma_sem"] is None and o["fn"] is not None:
                    c += 1
                cl.append(c)
            cum[e] = cl
            n = [None] * len(ops)
            last = None
            for i in range(len(ops) - 1, -1, -1):
                o = ops[i]
                if o["signal"] and o["dma_sem"] is None and o["fn"] is not None:
                    last = i
                n[i] = last
            nxt[e] = n

        def resolve(tok, gseq):
            if tok[0] == "d":
                return dmasem[tok[1]], tok[2], dmaop[(tok[1], tok[2])]
            e, i = tok[1], tok[2]
            j = nxt[e][i]
            assert j is not None, ("no signal after", tok)
            assert self.ops[e][j]["gseq"] < gseq, ("forward wait", tok, j)
            return engsem[e], cum[e][j], self.ops[e][j]

        allops = []
        for e in self.ENGS:
            for i, o in enumerate(self.ops[e]):
                o["eng"] = e
                o["idx"] = i
                allops.append(o)
        allops.sort(key=lambda x: x["gseq"])
        dmaop = {}
        dcount = {}
        for o in allops:
            ds = o["dma_sem"]
            if ds is not None:
                dcount[ds] = dcount.get(ds, 0) + 16
                o["dval"] = dcount[ds]
                dmaop[(ds, dcount[ds])] = o
        run_clock = {e: {} for e in self.ENGS}
        for o in allops:
            e = o["eng"]
            clk = run_clock[e]
            deps = []
            for t in o["deps"]:
                sem, val, prod = resolve(t, o["gseq"])
                deps.append((prod["gseq"], sem, val, prod))
            deps.sort(key=lambda x: -x[0])
            waits = []
            for _, sem, val, prod in deps:
                if clk.get(sem.num, 0) >= val:
                    continue
                waits.append((sem, val))
                for k, v in prod["tclock"].items():
                    if clk.get(k, 0) < v:
                        clk[k] = v
                if clk.get(sem.num, 0) < val:
                    clk[sem.num] = val
            o["waits"] = waits
            tc = dict(clk)
            if o["dma_sem"] is not None:
                tc[dmasem[o["dma_sem"]].num] = o["dval"]
            elif e != "sp" and o["fn"] is not None and o["signal"]:
                tc[engsem[e].num] = cum[e][o["idx"]]
            o["tclock"] = tc

        engobj = {"pe": "tensor", "act": "scalar", "dve": "vector", "pool": "gpsimd", "sp": "sync"}
        prog = self

        def run(ename, eng):
            for o in prog.ops[ename]:
                todo = list(o["waits"])
                attach = None
                if o["fn"] is not None and todo and ATTACH_WAIT:
                    attach = todo.pop(0)
                for sem, val in todo:
                    eng.wait_ge(sem, val)
                if o["fn"] is None:
                    continue
                ins = o["fn"](eng)
                if attach is not None:
                    ins._wait_ge(attach[0], attach[1])
                if o["dma_sem"] is not None:
                    ins.then_inc(dmasem[o["dma_sem"]], 16)
                elif o["signal"] and ename != "sp":
                    ins.then_inc(engsem[ename], 1)

        with nc.Block() as block:
            @block.tensor
            def _(e):
                run("pe", e)

            @block.scalar
            def _(e):
                run("act", e)

            @block.vector
            def _(e):
                run("dve", e)

            @block.gpsimd
            def _(e):
                run("pool", e)

            @block.sync
            def _(e):
                run("sp", e)


class Tile:
    def __init__(self, nc, name, shape, dtype):
        self.t = nc.alloc_sbuf_tensor(name, list(shape), dtype)
        self.name = name
        self.bufs = {}

    def buf(self, *key):
        b = self.bufs.get(key)
        if b is None:
            b = Buf(f"{self.name}{key}")
            self.bufs[key] = b
        return b


class Ring:
    def __init__(self, nc, name, n, shape, dtype):
        self.tiles = [Tile(nc, f"{name}{i}", shape, dtype) for i in range(n)]
        self.i = 0

    def next(self):
        t = self.tiles[self.i % len(self.tiles)]
        self.i += 1
        return t


def _fm(v):
    v = np.asarray(v, np.float32).reshape(-1, 128)
    return np.ascontiguousarray(v.T)


def _smallp_layout():
    cols = {}
    off = 0

    def add(name, n):
        nonlocal off
        cols[name] = off
        off += n

    add("c", 8)
    for l in range(DEPTH):
        for i in range(4):
            add(f"ng{l}_{i}", 8)
    for l in range(DEPTH):
        add(f"adab{l}", 48)
    for l in range(N_A):
        for t in range(3):
            add(f"cw{l}_{t}", 8)
        add(f"cb{l}", 8)
    add("kvadab", 16)
    add("kvng", 8)
    add("kvang", 2)
    for j in range(DEPTH - N_A):
        add(f"qang{j}", 3)
    return cols, off


SP_COLS, SP_N = _smallp_layout()


def _rope_tables():
    inv_freq = 1.0 / (10000.0 ** (np.arange(0, QK_ROPE, 2, dtype=np.float32) / QK_ROPE))
    ang = np.arange(S, dtype=np.float32)[:, None] * inv_freq[None, :]
    cos = np.cos(ang).astype(np.float32).T
    sin = np.sin(ang).astype(np.float32).T
    cos128 = np.concatenate([cos, cos, cos, cos], 0)
    sins = np.concatenate([-sin, sin, -sin, sin], 0)
    return np.ascontiguousarray(np.stack([cos128, sins], 0))


def build(n_layers=DEPTH, n_groups=NG, dbg=None):
    nc = bass.Bass("TRN2", target_bir_lowering=False)
    P = Prog()

    def din(name, shape):
        return nc.dram_tensor(name, list(shape), F32, kind="ExternalInput").ap()

    xT = din("xT", [D, S])
    smallp_d = din("smallp", [128, SP_N])
    rope_d = din("rope", [2, 128, S])
    ident_d = din("ident", [128, 128])
    ada_w = din("ada_w", [DEPTH, D, 6 * D])
    kv_ada_w = din("kv_ada_w", [D, 2 * D])
    conv_in_r = din("conv_in_r", [N_A, D, 3 * D])
    conv_out_w = din("conv_out_w", [N_A, D, D])
    kv_a_ext = din("kv_a_ext", [D, 512])
    kv_b_w = din("kv_b_w", [KV_LORA, 2048])
    q_a_w = din("q_a_w", [2, D, Q_LORA])
    q_b_ext = din("q_b_ext", [2, Q_LORA, 2048])
    attn_o_w = din("attn_o_w", [2, D, D])
    mlp_up_w = din("mlp_up_w", [DEPTH, D, DFF])
    mlp_down_w = din("mlp_down_w", [DEPTH, DFF, D])
    outT = nc.dram_tensor("outT", [D, S], F32, kind="ExternalOutput").ap()

    X = Tile(nc, "X", [128, DC, GT], F32)
    HB = Tile(nc, "HB", [128, DC, GT], BF16)
    Y = Tile(nc, "Y", [128, DC, GT], F32)
    A = Tile(nc, "A", [128, DFF // 128, GT], BF16)
    WR = [Tile(nc, f"WR{i}", [128, WSLOT], BF16) for i in range(NSLOT)]
    SQ = Ring(nc, "SQ", 3, [128, GT], BF16)
    RS = Ring(nc, "RS", 2, [128, GT], F32)
    TMP = Ring(nc, "TMP", 3, [128, GT], F32)
    UU = Ring(nc, "UU", 2, [128, GT + 2], F32)
    CKV = Tile(nc, "CKV", [128, 2, S], BF16)
    KPE = Tile(nc, "KPE", [128, S], BF16)
    ROPE = Tile(nc, "ROPE", [128, 2, GT], F32)
    QLB = Tile(nc, "QLB", [128, 3, GT], BF16)
    QNR = [Tile(nc, f"QN{i}", [128, 2, GT], BF16) for i in range(2)]
    QRBDR = [Tile(nc, f"QRBD{i}", [128, GT // 128, 256], BF16) for i in range(2)]
    KT = Tile(nc, "KT", [128, 2, S], BF16)
    VA = Tile(nc, "VA", [128, S // 128, 2, 130], BF16)
    PT = Ring(nc, "PT", 3, [128, 512], BF16)
    PTD = Ring(nc, "PTD", 2, [128, 256], BF16)
    OB = Ring(nc, "OB", 4, [128, 128], BF16)
    RINV = Ring(nc, "RINV", 4, [128, 1], F32)
    ADAST = Ring(nc, "ADAST", 4, [128, 1024], F32)
    ACCR = Ring(nc, "ACC", 2, [128, 2048], F32)
    SMP = Tile(nc, "SMP", [128, SP_N], F32)
    CACT = Tile(nc, "CACT", [128, 8], F32)
    MOD = Tile(nc, "MOD", [128, DEPTH, 48], F32)
    KVMOD = Tile(nc, "KVMOD", [128, 16], F32)
    DER = Tile(nc, "DER", [128, DEPTH, 4, 8], F32)
    DKV = Tile(nc, "DKV", [128, 16], F32)
    HALO = Tile(nc, "HALO", [128, N_A, DC, 2], F32)
    ONESB = Tile(nc, "ONESB", [128, 128], BF16)
    ONESF = Tile(nc, "ONESF", [128, 1], F32)
    IDB = Tile(nc, "IDB", [128, 128], BF16)
    FILL = Tile(nc, "FILL", [128, 512], BF16)

    NPSM = 3
    PSM = [nc.alloc_psum_tensor(f"psm{i}", [128, 512], F32) for i in range(NPSM)]
    PSMb = [Buf(f"psm{i}") for i in range(NPSM)]
    NPSN = 1
    PSN = [nc.alloc_psum_tensor(f"psn{i}", [128, 512], F32) for i in range(NPSN)]
    PSNb = [Buf(f"psn{i}") for i in range(NPSN)]
    PST = [nc.alloc_psum_tensor(f"pst{i}", [128, 512], F32) for i in range(2)]
    PSTb = [Buf(f"pst{i}") for i in range(2)]
    PSOT = [nc.alloc_psum_tensor(f"pso{i}", [128, 512], F32) for i in range(2)]
    PSOb = [Buf(f"pso{i}") for i in range(2)]
    PSTR = [[PSOT[i][:, 264 + 64 * h:264 + 64 * (h + 1)].bitcast(BF16) for h in range(2)]
            for i in range(2)]
    PSA = PSOT[0][:, 400:464]
    PSAb = PSOb[0]
    cnt = {"psm": 0, "psn": 0, "pst": 0, "pso": 0, "pstr": 0}

    def psm_next():
        i = cnt["psm"] % NPSM
        cnt["psm"] += 1
        return PSM[i], PSMb[i]

    def mm(out, lhsT, rhs, start, stop, reads, writes, signal=None, sgc=False):
        if sgc:
            P.emit("pe", lambda e: e.matmul(out, lhsT, rhs, start=start, stop=stop,
                                            skip_group_check=True),
                   reads, writes, signal=(stop if signal is None else signal))
        else:
            P.emit("pe", lambda e: e.matmul(out, lhsT, rhs, start=start, stop=stop),
                   reads, writes, signal=(stop if signal is None else signal))

    def act(out, in_, func, reads, writes, bias=None, scale=None):
        kw = {}
        if bias is not None:
            kw["bias"] = bias
        if scale is not None:
            kw["scale"] = scale
        P.emit("act", lambda e: e.activation(out, in_, func, **kw), reads, writes)

    def tt(eng, out, in0, in1, op, reads, writes):
        P.emit(eng, lambda e: e.tensor_tensor(out, in0, in1, op), reads, writes)

    def ts(eng, out, in0, s1, s2, op0, op1, reads, writes):
        if op1 is None:
            P.emit(eng, lambda e: e.tensor_scalar(out, in0, s1, None, op0), reads, writes)
        else:
            P.emit(eng, lambda e: e.tensor_scalar(out, in0, s1, s2, op0, op1), reads, writes)

    def stt(eng, out, in0, scalar, in1, op0, op1, reads, writes):
        P.emit(eng, lambda e: e.scalar_tensor_tensor(out, in0, scalar, in1, op0, op1),
               reads, writes)

    def dma(eng, out, in_, reads, writes, sem):
        P.emit(eng, lambda e: e.dma_start(out=out, in_=in_), reads, writes, dma_sem=sem)

    def memset(eng, ap, val, writes):
        P.emit(eng, lambda e: e.memset(ap, val), (), writes)

    def spc(name, n=8, off=0):
        c0 = SP_COLS[name] + off
        return SMP.t[:, c0:c0 + n]

    SMPb = SMP.buf()

    def wsrc3(w2d, kc, c0, c1):
        return w2d.rearrange("(kc p) n -> p kc n", p=128)[:, 0:kc, c0:c1]

    def weight_schedule():
        for g in range(n_groups):
            for l in range(n_layers):
                if l < N_A:
                    for c in range(DC):
                        yield (f"cin{l}_{c}", conv_in_r[l], 8, c * 384, (c + 1) * 384, 1)
                    for i in range(2):
                        yield (f"cout{l}_{i}", conv_out_w[l], 8, i * 512, (i + 1) * 512, 1)
                else:
                    j = l - N_A
                    if l == N_A:
                        yield ("kva", kv_a_ext, 8, 0, 512, 1)
                    yield (f"kvb{j}", kv_b_w, 2, 0, 2048, 1)
                    yield (f"qa{j}", q_a_w[j], 8, 0, 384, 1)
                    for i in range(2):
                        yield (f"qb{j}_{i}", q_b_ext[j], 3, i * 1024, (i + 1) * 1024, 1)
                    for i in range(2):
                        yield (f"ow{j}_{i}", attn_o_w[j], 8, i * 512, (i + 1) * 512, 1)
                for i in range(8):
                    yield (f"up{l}_{i}", mlp_up_w[l], 8, i * 512, (i + 1) * 512, 1)
                for i in range(8):
                    yield (f"dn{l}_{i}", mlp_down_w[l], 32, i * 128, (i + 1) * 128, 2)

    wsched = list(weight_schedule())
    wstate = {"next": 0}
    WRb = [WR[i].buf() for i in range(NSLOT)]
    ntg = len(wsched) // n_groups
    use_cache = n_groups > 1
    if use_cache:
        wcache = nc.dram_tensor("wcache", [ntg, 128, WSLOT], BF16).ap()
        SCRb = [Buf(f"scr{t}") for t in range(ntg)]

    def w_issue(i):
        if i >= len(wsched):
            return
        name, w2d, kc, c0, c1, nsplit = wsched[i]
        slot = i % NSLOT
        ncols = c1 - c0
        if use_cache and i >= ntg:
            t = i % ntg
            dma("pool", WR[slot].t[:, 0:kc * ncols], wcache[t][:, 0:kc * ncols], [SCRb[t]],
                [WRb[slot]], f"w{slot}_{(i // NSLOT) % 3}")
            return
        dst = WR[slot].t[:, 0:kc * ncols].rearrange("p (k n) -> p k n", k=kc)
        src = wsrc3(w2d, kc, c0, c1)
        ks = kc // nsplit
        for s in range(nsplit):
            dma("pool", dst[:, s * ks:(s + 1) * ks, :], src[:, s * ks:(s + 1) * ks, :],
                (), [WRb[slot]], f"w{slot}_{(i // NSLOT) % 3}")

    def w_writeback(i):
        if not use_cache or i >= ntg:
            return
        name, w2d, kc, c0, c1, nsplit = wsched[i]
        slot = i % NSLOT
        n = kc * (c1 - c0)
        dma("sp", wcache[i][:, 0:n], WR[slot].t[:, 0:n], [WRb[slot]], [SCRb[i]], f"ws{slot}")

    def w_acquire(name):
        i = wstate["next"]
        wstate["next"] += 1
        spec = wsched[i]
        assert spec[0] == name, (spec[0], name)
        kc, ncols = spec[2], spec[4] - spec[3]
        slot = i % NSLOT
        view = WR[slot].t[:, 0:kc * ncols].rearrange("p (k n) -> p k n", k=kc)
        return i, view, WRb[slot]

    def w_release(i):
        w_writeback(i)
        w_issue(i + NSLOT)
        pump(2)

    dma("sp", SMP.t[:, :], smallp_d, (), [SMPb], "smp")
    memset("dve", ONESB.t[:, :], 1.0, [ONESB.buf()])
    memset("dve", ONESF.t[:, :], 1.0, [ONESF.buf()])
    memset("dve", VA.t[:, :, :, :], 1.0, [VA.buf()])
    for r in PTD.tiles:
        memset("dve", r.t[:, :], 0.0, [r.buf()])
    for r in QRBDR:
        memset("dve", r.t[:, :, :], 0.0, [r.buf()])
    memset("dve", FILL.t[:, :], 0.0, [FILL.buf()])
    memset("dve", HALO.t[:, :, :, :], 0.0, [HALO.buf(l, c) for l in range(N_A) for c in range(DC)])
    dma("pool", IDB.t[:, :], ident_d, (), [IDB.buf()], "ident")
    for i in range(NSLOT):
        w_issue(i)
    act(CACT.t[:, :], spc("c"), AF.Silu, [SMPb], [CACT.buf()])

    micro = []

    def ada_part(w2d, col0, bias_cols, out_ap, out_bufs):
        st_ = {}

        def kstep(kc):
            if kc == 0:
                st_["acc"] = ACCR.next()
            acc = st_["acc"]
            for hf in range(2):
                st = ADAST.next()
                c0 = col0 + hf * 1024
                dma("sp", st.t[:, :], w2d[kc * 128:(kc + 1) * 128, c0:c0 + 1024], (),
                    [st.buf()], f"ada{(ADAST.i - 1) % 4}")
                av = acc.t[:, hf * 1024:(hf + 1) * 1024]
                if kc == 0:
                    ts("dve", av, st.t[:, :], CACT.t[:, 0:1], None, ALU.mult, None,
                       [st.buf(), CACT.buf()], [acc.buf(hf)])
                else:
                    stt("dve", av, st.t[:, :], CACT.t[:, kc:kc + 1], av, ALU.mult, ALU.add,
                        [st.buf(), CACT.buf(), acc.buf(hf)], [acc.buf(hf)])

        def red():
            acc = st_["acc"]
            for i in range(16):
                mm(PSA[:, i:i + 1], acc.t[:, i * 128:(i + 1) * 128], ONESF.t[:, 0:1], True, True,
                   [acc.buf(i // 8), ONESF.buf()], [PSAb], signal=(i == 15))
            tt("dve", out_ap, PSA[:, 0:16], bias_cols, ALU.add, [PSAb, SMPb], out_bufs)

        return [(lambda kc=kc: kstep(kc)) for kc in range(8)] + [red]

    def schedule_parts(parts, tail):
        out = []
        prev = None
        for p_ in parts:
            out += p_[0:8]
            if prev is not None:
                out.append(prev)
            prev = p_[8]
        out.append(prev)
        return out + tail

    def ada_layer_parts(l):
        return [ada_part(ada_w[l], j * 1024, spc(f"adab{l}", 16, j * 8),
                         MOD.t[:, l, j * 8:(j + 2) * 8], [MOD.buf(l, j), MOD.buf(l, j + 1)])
                for j in (0, 2, 4)]

    def kv_parts():
        return [ada_part(kv_ada_w, 0, spc("kvadab", 16, 0), KVMOD.t[:, 0:16],
                         [KVMOD.buf(0), KVMOD.buf(1)])]

    def pump(k):
        for _ in range(k):
            if micro:
                micro.pop(0)()

    derived = set()

    def derive_k(l, k):
        if k in (0, 2):
            jsc, gi = (1, 0) if k == 0 else (4, 2)
            ts("dve", DER.t[:, l, k, :], MOD.t[:, l, jsc * 8:(jsc + 1) * 8], 1.0, 32.0, ALU.add,
               ALU.mult, [MOD.buf(l, jsc)], [DER.buf(l, k)])
            tt("dve", DER.t[:, l, k, :], DER.t[:, l, k, :], spc(f"ng{l}_{gi}"), ALU.mult,
               [DER.buf(l, k), SMPb], [DER.buf(l, k)])
        else:
            jg, gi = (2, 1) if k == 1 else (5, 3)
            stt("dve", DER.t[:, l, k, :], MOD.t[:, l, jg * 8:(jg + 1) * 8], 32.0,
                spc(f"ng{l}_{gi}"), ALU.mult, ALU.mult, [MOD.buf(l, jg), SMPb], [DER.buf(l, k)])
        derived.add((l, k))

    def derive_layer(l):
        for k in range(4):
            derive_k(l, k)

    def ensure(l, k):
        while (l, k) not in derived:
            assert micro, ("mods not scheduled", l, k)
            micro.pop(0)()

    def derive_kv():
        ts("dve", DKV.t[:, 0:8], KVMOD.t[:, 8:16], 1.0, 32.0, ALU.add, ALU.mult,
           [KVMOD.buf(1)], [DKV.buf(0)])
        tt("dve", DKV.t[:, 0:8], DKV.t[:, 0:8], spc("kvng"), ALU.mult, [DKV.buf(0), SMPb],
           [DKV.buf(0)])
        ts("dve", DKV.t[:, 8:10], spc("kvang", 2), 16.0, None, ALU.mult, None, [SMPb],
           [DKV.buf(1)])
        for j in range(2):
            ts("dve", DKV.t[:, 10 + 3 * j:13 + 3 * j], spc(f"qang{j}", 3),
               float(np.sqrt(Q_LORA)) * SM_SCALE, None, ALU.mult, None, [SMPb], [DKV.buf(2 + j)])

    def sq_accum(src_ap, src_bufs, idx, total, kparts=128):
        sq = SQ.next()
        act(sq.t[0:kparts, :], src_ap, AF.Square, src_bufs, [sq.buf()])
        i = cnt["psn"] % NPSN
        mm(PSN[i][:, :], ONESB.t[0:kparts, :], sq.t[0:kparts, :], idx == 0, idx == total - 1,
           [ONESB.buf(), sq.buf()], [PSNb[i]], signal=True)

    def pe_fill(n):
        if n <= 0:
            return
        ps, psb = psm_next()
        for i in range(n):
            mm(ps[:, :], ONESB.t[:, :], FILL.t[:, :], True, True, [ONESB.buf(), FILL.buf()],
               [psb], signal=(i == n - 1))

    def rs_finish(n_eps):
        i = cnt["psn"] % NPSN
        cnt["psn"] += 1
        rs = RS.next()
        act(rs.t[:, :], PSN[i][:, :], AF.Sqrt, [PSNb[i]], [rs.buf()], bias=float(n_eps))
        P.emit("dve", lambda e: e.reciprocal(rs.t[:, :], rs.t[:, :]), [rs.buf()], [rs.buf()])
        return rs

    def norm_mod(scale_ap_fn, shift_ap_fn, sbufs, rs=None):
        if rs is None:
            pe_fill(NFILL)
            for c in range(DC):
                sq_accum(X.t[:, c, :], [X.buf(c)], c, DC)
            rs = rs_finish(D * EPS)
        for c in range(DC):
            tm = TMP.next()
            tt("dve", tm.t[:, :], X.t[:, c, :], rs.t[:, :], ALU.mult, [X.buf(c), rs.buf()],
               [tm.buf()])
            act(HB.t[:, c, :], tm.t[:, :], AF.Identity, [tm.buf()] + sbufs, [HB.buf(c)],
                bias=shift_ap_fn(c), scale=scale_ap_fn(c))
        return rs

    def resid(l, k):
        rs = rs_finish(D * EPS)
        prev = None
        for c in range(DC + 1):
            cur = None
            if c < DC:
                tm = TMP.next()
                tt("dve", tm.t[:, :], Y.t[:, c, :], rs.t[:, :], ALU.mult, [Y.buf(c), rs.buf()],
                   [tm.buf()])
                cur = (c, tm)
            if prev is not None:
                pc, ptm = prev
                tt("dve", X.t[:, pc, :], X.t[:, pc, :], ptm.t[:, :], ALU.add,
                   [ptm.buf(), X.buf(pc)], [X.buf(pc)])
            prev = cur

    def y_evac(l, k, n, ps, psb):
        ensure(l, k)
        sq_accum(ps[:, :], [psb], n, DC)
        act(Y.t[:, n, :], ps[:, :], AF.Identity, [psb, DER.buf(l, k)], [Y.buf(n)],
            scale=DER.t[:, l, k, n:n + 1])

    def linear(in_fn, in_bufs_fn, KC, tiles, consumer, kparts=128, stream_in=False):
        n = 0
        tiles = list(tiles)
        done = {}
        if stream_in:
            heads = []
            held = []
            tpos = 0
            while len(heads) < NPSM and tpos < len(tiles):
                tname, nper = tiles[tpos]
                ti, wv, wb = w_acquire(tname)
                take = min(nper, NPSM - len(heads))
                for nn in range(take):
                    heads.append((wv, wb, nn))
                held.append((tpos, ti, wv, wb, take))
                tpos += 1
            pss = [psm_next() for _ in heads]
            for kc in range(KC):
                for hi_, (wv, wb, nn) in enumerate(heads):
                    ps, psb = pss[hi_]
                    mm(ps[:, :], wv[0:kparts, kc, nn * 128:(nn + 1) * 128], in_fn(kc),
                       kc == 0, kc == KC - 1, [wb] + in_bufs_fn(kc), [psb], signal=True)
            for hi_ in range(len(heads)):
                consumer(n, pss[hi_][0], pss[hi_][1])
                n += 1
            for (tp, ti, wv, wb, take) in held:
                done[tp] = (ti, wv, wb, take)
        for tpos, (tname, nper) in enumerate(tiles):
            if tpos in done:
                ti, wv, wb, nn0 = done[tpos]
            else:
                ti, wv, wb = w_acquire(tname)
                nn0 = 0
            for nn in range(nn0, nper):
                ps, psb = psm_next()
                for kc in range(KC):
                    mm(ps[:, :], wv[0:kparts, kc, nn * 128:(nn + 1) * 128], in_fn(kc),
                       kc == 0, kc == KC - 1, [wb] + in_bufs_fn(kc), [psb])
                consumer(n, ps, psb)
                n += 1
            w_release(ti)

    def conv_mixer(l, g):
        ensure(l, 0)
        def in_fn(kc):
            return HB.t[:, kc, :]

        def in_bufs(kc):
            return [HB.buf(kc)]

        norm_mod(lambda c: DER.t[:, l, 0, c:c + 1], lambda c: MOD.t[:, l, 0 + c:c + 1],
                 [DER.buf(l, 0), MOD.buf(l, 0)])
        for c in range(DC):
            ti, wv, wb = w_acquire(f"cin{l}_{c}")
            pss = []
            if c == 0:
                pss = [psm_next() for _ in range(3)]
                for kc in range(DC):
                    for part in range(3):
                        mm(pss[part][0][:, :], wv[:, kc, part * 128:(part + 1) * 128],
                           HB.t[:, kc, :], kc == 0, kc == DC - 1, [wb, HB.buf(kc)],
                           [pss[part][1]], signal=True)
            for part in range(3):
                if c == 0:
                    ps, psb = pss[part]
                else:
                    ps, psb = psm_next()
                    for kc in range(DC):
                        mm(ps[:, :], wv[:, kc, part * 128:(part + 1) * 128], HB.t[:, kc, :],
                           kc == 0, kc == DC - 1, [wb, HB.buf(kc)], [psb])
                if part == 0:
                    cg = TMP.next()
                    act(cg.t[:, :], ps[:, :], AF.Identity, [psb], [cg.buf()])
                elif part == 1:
                    u = UU.next()
                    act(u.t[:, 0:2], HALO.t[:, l, c, :], AF.Identity, [HALO.buf(l, c)], [u.buf()])
                    tt("dve", u.t[:, 2:GT + 2], cg.t[:, :], ps[:, :], ALU.mult, [cg.buf(), psb],
                       [u.buf()])
                    act(HALO.t[:, l, c, :], u.t[:, GT:GT + 2], AF.Identity, [u.buf()],
                        [HALO.buf(l, c)])
                    cv = TMP.next()
                    ts("dve", cv.t[:, :], u.t[:, 2:GT + 2], spc(f"cw{l}_2", 1, c),
                       spc(f"cb{l}", 1, c), ALU.mult, ALU.add, [u.buf(), SMPb], [cv.buf()])
                    stt("dve", cv.t[:, :], u.t[:, 1:GT + 1], spc(f"cw{l}_1", 1, c), cv.t[:, :],
                        ALU.mult, ALU.add, [u.buf(), SMPb, cv.buf()], [cv.buf()])
                    stt("dve", cv.t[:, :], u.t[:, 0:GT], spc(f"cw{l}_0", 1, c), cv.t[:, :],
                        ALU.mult, ALU.add, [u.buf(), SMPb, cv.buf()], [cv.buf()])
                else:
                    tt("dve", A.t[:, c, :], ps[:, :], cv.t[:, :], ALU.mult, [psb, cv.buf()],
                       [A.buf(c)])
            w_release(ti)

        linear(lambda kc: A.t[:, kc, :], lambda kc: [A.buf(kc)], DC,
               [(f"cout{l}_0", 4), (f"cout{l}_1", 4)],
               lambda n, ps, psb: y_evac(l, 1, n, ps, psb), stream_in=True)
        resid(l, 1)

    def mlp(l, g):
        ensure(l, 2)
        ensure(l, 3)
        norm_mod(lambda c: DER.t[:, l, 2, c:c + 1], lambda c: MOD.t[:, l, 24 + c:25 + c],
                 [DER.buf(l, 2), MOD.buf(l, 3)])

        def cons_up(n, ps, psb):
            r = TMP.next()
            act(r.t[:, :], ps[:, :], AF.Relu, [psb], [r.buf()])
            act(A.t[:, n, :], r.t[:, :], AF.Square, [r.buf()], [A.buf(n)])

        linear(lambda kc: HB.t[:, kc, :], lambda kc: [HB.buf(kc)], DC,
               [(f"up{l}_{i}", 4) for i in range(8)], cons_up, stream_in=True)

        linear(lambda kc: A.t[:, kc, :], lambda kc: [A.buf(kc)], DFF // 128,
               [(f"dn{l}_{i}", 1) for i in range(8)],
               lambda n, ps, psb: y_evac(l, 3, n, ps, psb))
        resid(l, 3)

    def shared_kv(g, rs):
        t0 = g * GT
        norm_mod(lambda c: DKV.t[:, c:c + 1], lambda c: KVMOD.t[:, c:c + 1],
                 [DKV.buf(0), KVMOD.buf(0)], rs=rs)
        lat = {}

        def cons(n, ps, psb):
            if n < 2:
                act(Y.t[:, n, :], ps[:, :], AF.Identity, [psb], [Y.buf(n)])
                sq_accum(ps[:, :], [psb], n, 2)
                lat[n] = True
            elif n == 2:
                tm = TMP.next()
                tt("dve", tm.t[:, :], ps[:, :], ROPE.t[:, 0, :], ALU.mult, [psb, ROPE.buf()],
                   [tm.buf()])
                lat["pe"] = tm
            else:
                tm2 = TMP.next()
                tt("dve", tm2.t[:, :], ps[:, :], ROPE.t[:, 1, :], ALU.mult, [psb, ROPE.buf()],
                   [tm2.buf()])
                tm = lat["pe"]
                tt("dve", KPE.t[:, t0:t0 + GT], tm.t[:, :], tm2.t[:, :], ALU.add,
                   [tm.buf(), tm2.buf()], [KPE.buf(g)])

        linear(lambda kc: HB.t[:, kc, :], lambda kc: [HB.buf(kc)], DC, [("kva", 4)], cons,
               stream_in=True)
        rsk = rs_finish(KV_LORA * EPS)
        for n in range(2):
            tm = TMP.next()
            tt("dve", tm.t[:, :], Y.t[:, n, :], rsk.t[:, :], ALU.mult, [Y.buf(n), rsk.buf()],
               [tm.buf()])
            act(CKV.t[:, n, t0:t0 + GT], tm.t[:, :], AF.Identity, [tm.buf(), DKV.buf(1)],
                [CKV.buf(n, g)], scale=DKV.t[:, 8 + n:9 + n])

    def attention(l, g, rs_in):
        j = l - N_A
        t0 = g * GT
        kend = t0 + GT
        nkc = kend // 128
        tkv, wkv, wkvb = w_acquire(f"kvb{j}")

        def kv_pair(hp):
            for hi in range(2):
                h = 2 * hp + hi
                for kg in range(kend // 512):
                    ps, psb = psm_next()
                    for kc in range(2):
                        mm(ps[:, :], wkv[:, kc, h * 256:h * 256 + 128],
                           CKV.t[:, kc, kg * 512:(kg + 1) * 512], kc == 0, kc == 1,
                           [wkvb, CKV.buf(kc, kg)], [psb])
                    act(KT.t[:, hi, kg * 512:(kg + 1) * 512], ps[:, :], AF.Identity, [psb],
                        [KT.buf(hi, kg)])
            for tc4 in range(nkc // 2):
                ps, psb = psm_next()
                for sub in range(2):
                    tcn = tc4 * 2 + sub
                    for hi in range(2):
                        h = 2 * hp + hi
                        for kc in range(2):
                            mm(ps[:, sub * 256 + hi * 128:sub * 256 + (hi + 1) * 128],
                               CKV.t[:, kc, tcn * 128:(tcn + 1) * 128],
                               wkv[:, kc, h * 256 + 128:h * 256 + 256], kc == 0, kc == 1,
                               [wkvb, CKV.buf(kc, tcn // 4)], [psb],
                               signal=(kc == 1 and hi == 1 and sub == 1))
                for sub in range(2):
                    tcn = tc4 * 2 + sub
                    P.emit("dve", lambda e, tcn=tcn, sub=sub, ps=ps: e.tensor_copy(
                        VA.t[:, tcn, :, 0:128],
                        ps[:, sub * 256:(sub + 1) * 256].rearrange("p (h d) -> p h d", h=2)),
                        [psb], [VA.buf(tcn)])

        kv_pair(0)
        ensure(l, 0)
        norm_mod(lambda c: DER.t[:, l, 0, c:c + 1], lambda c: MOD.t[:, l, 0 + c:c + 1],
                 [DER.buf(l, 0), MOD.buf(l, 0)], rs=rs_in)

        def cons_qa(n, ps, psb):
            act(Y.t[:, n, :], ps[:, :], AF.Identity, [psb], [Y.buf(n)])
            sq_accum(ps[:, :], [psb], n, 3)

        linear(lambda kc: HB.t[:, kc, :], lambda kc: [HB.buf(kc)], DC, [(f"qa{j}", 3)], cons_qa,
               stream_in=True)
        rsq = rs_finish(Q_LORA * EPS)
        for n in range(3):
            tm = TMP.next()
            tt("dve", tm.t[:, :], Y.t[:, n, :], rsq.t[:, :], ALU.mult, [Y.buf(n), rsq.buf()],
               [tm.buf()])
            act(QLB.t[:, n, :], tm.t[:, :], AF.Identity, [tm.buf(), DKV.buf(2 + j)], [QLB.buf(n)],
                scale=DKV.t[:, 10 + 3 * j + n:11 + 3 * j + n])
        qst = {"tq": None}

        def q_proj(hp):
            QN = QNR[hp % 2]
            QRBD = QRBDR[hp % 2]
            if hp % 2 == 0:
                qst["tq"] = w_acquire(f"qb{j}_{hp // 2}")
            tq, wq, wqb = qst["tq"]
            cb = (hp % 2) * 512
            for part in range(4):
                ps, psb = psm_next()
                for kc in range(3):
                    mm(ps[:, :], wq[:, kc, cb + part * 128:cb + (part + 1) * 128],
                       QLB.t[:, kc, :], kc == 0, kc == 2, [wqb, QLB.buf(kc)], [psb])
                if part < 2:
                    act(QN.t[:, part, :], ps[:, :], AF.Identity, [psb], [QN.buf(part)])
                elif part == 2:
                    tpe = TMP.next()
                    tt("dve", tpe.t[:, :], ps[:, :], ROPE.t[:, 0, :], ALU.mult,
                       [psb, ROPE.buf()], [tpe.buf()])
                else:
                    tsw = TMP.next()
                    tt("dve", tsw.t[:, :], ps[:, :], ROPE.t[:, 1, :], ALU.mult,
                       [psb, ROPE.buf()], [tsw.buf()])
                    for hh in range(2):
                        r0 = hh * 64
                        tt("dve", QRBD.t[r0:r0 + 64, :, hh * 128:(hh + 1) * 128],
                           tpe.t[r0:r0 + 64, :].rearrange("p (q t) -> p q t", t=128),
                           tsw.t[r0:r0 + 64, :].rearrange("p (q t) -> p q t", t=128), ALU.add,
                           [tpe.buf(), tsw.buf()], [QRBD.buf()])
            if hp % 2 == 1:
                w_release(tq)

        q_proj(0)
        for hp in range(4):
            if hp < 3:
                q_proj(hp + 1)
            QN = QNR[hp % 2]
            QRBD = QRBDR[hp % 2]
            items = []
            for qb in range(GT // 128):
                qbg = t0 // 128 + qb
                kcs = list(range(qbg))
                bts = [kcs[b0:b0 + 2] for b0 in range(0, len(kcs), 2)] + [[qbg]]
                step = 0
                for bi_, bt in enumerate(bts):
                    items.append(dict(qb=qb, qbg=qbg, bt=bt, diag=(bt[0] == qbg), step0=step,
                                      nsteps=qbg + 1, last=(bi_ == len(bts) - 1),
                                      first=(bi_ == 0)))
                    step += len(bt)
            psomap = {}

            def emit_st(it):
                qb, bt = it["qb"], it["bt"]
                si = cnt["pst"] % 2
                cnt["pst"] += 1
                pst, pstb = PST[si], PSTb[si]
                it["pst"] = (pst, pstb)
                for bi, kc in enumerate(bt):
                    c0 = bi * 256
                    mm(pst[:, c0:c0 + 256], KPE.t[:, kc * 128:(kc + 1) * 128], QRBD.t[:, qb, :],
                       True, False, [KPE.buf(kc // 4), QRBD.buf()], [pstb], signal=False,
                       sgc=True)
                    for hi in range(2):
                        mm(pst[:, c0 + hi * 128:c0 + (hi + 1) * 128],
                           KT.t[:, hi, kc * 128:(kc + 1) * 128],
                           QN.t[:, hi, qb * 128:(qb + 1) * 128], False, hi == 1,
                           [KT.buf(hi, kc // 4), QN.buf(hi)], [pstb],
                           signal=(hi == 1 and bi == len(bt) - 1), sgc=True)

            def emit_exp(it):
                pst, pstb = it["pst"]
                if it["diag"]:
                    pt = PTD.next()
                    act(pt.t[0:64, :], pst[0:64, 0:256], AF.Exp, [pstb], [pt.buf()])
                    ov = pt.t[64:128, :].rearrange("p (h c j) -> p h c j", h=2, c=2)[:, :, 1, :]
                    iv = pst[64:128, 0:256].rearrange("p (h c j) -> p h c j", h=2, c=2)[:, :, 1, :]
                    act(ov, iv, AF.Exp, [pstb], [pt.buf()])
                else:
                    pt = PT.next()
                    w = 256 * len(it["bt"])
                    act(pt.t[:, 0:w], pst[:, 0:w], AF.Exp, [pstb], [pt.buf()])
                it["pt"] = pt

            def emit_pv(it):
                qb, bt = it["qb"], it["bt"]
                if it["first"]:
                    oi = cnt["pso"] % 2
                    cnt["pso"] += 1
                    psomap[qb] = oi
                oi = psomap[qb]
                psot, psob = PSOT[oi], PSOb[oi]
                pt = it["pt"]
                step = it["step0"]
                for bi, kc in enumerate(bt):
                    for hi in range(2):
                        mm(psot[:, hi * 132:hi * 132 + 129],
                           pt.t[:, bi * 256 + hi * 128:bi * 256 + (hi + 1) * 128],
                           VA.t[:, kc, hi, 0:129], step == 0 and hi == 0,
                           step == it["nsteps"] - 1, [pt.buf(), VA.buf(kc)], [psob],
                           signal=(bi == len(bt) - 1 and hi == 1), sgc=True)
                    step += 1

            def emit_fin_a(it):
                oi = psomap[it["qb"]]
                psot, psob = PSOT[oi], PSOb[oi]
                it["oi"] = oi
                obs = []
                rinvs = []
                for hi in range(2):
                    rinv = RINV.next()
                    P.emit("dve", lambda e, rinv=rinv, psot=psot, hi=hi: e.reciprocal(
                        rinv.t[:, :], psot[:, hi * 132 + 128:hi * 132 + 129]), [psob],
                        [rinv.buf()])
                    rinvs.append(rinv)
                for hi in range(2):
                    rinv = rinvs[hi]
                    ob = OB.next()
                    ts("dve", ob.t[:, :], psot[:, hi * 132:hi * 132 + 128], rinv.t[:, 0:1], None,
                       ALU.mult, None, [psob, rinv.buf()], [ob.buf()])
                    obs.append(ob)
                it["obs"] = obs

            def emit_fin_b(it):
                oi = it["oi"]
                qb = it["qb"]
                for hi in range(2):
                    ob = it["obs"][hi]
                    h = 2 * hp + hi
                    trp = PSTR[oi][hi]
                    P.emit("pe", lambda e, trp=trp, ob=ob: e.transpose(trp, ob.t[:, :],
                                                                       IDB.t[:, :]),
                           [ob.buf(), IDB.buf()], [PSOb[oi]])
                    act(A.t[:, h, qb * 128:(qb + 1) * 128], trp, AF.Identity, [PSOb[oi]],
                        [A.buf(h)])

            emit_st(items[0])
            if len(items) > 1:
                emit_st(items[1])
            pend = None
            for i_, it in enumerate(items):
                emit_exp(it)
                if i_ + 2 < len(items):
                    emit_st(items[i_ + 2])
                emit_pv(it)
                if pend is not None:
                    emit_fin_b(pend)
                    pend = None
                if it["last"]:
                    emit_fin_a(it)
                    pend = it
            emit_fin_b(pend)
            if hp < 3:
                kv_pair(hp + 1)
                if hp == 2:
                    w_release(tkv)

        linear(lambda kc: A.t[:, kc, :], lambda kc: [A.buf(kc)], DC,
               [(f"ow{j}_0", 4), (f"ow{j}_1", 4)],
               lambda n, ps, psb: y_evac(l, 1, n, ps, psb), stream_in=True)
        resid(l, 1)

    p0 = ada_layer_parts(0)
    for st_ in p0[0]:
        st_()
    derive_k(0, 0)
    micro.extend(p0[1][0:8] + [p0[1][8], lambda: derive_k(0, 1)] + p0[2][0:8]
                 + [p0[2][8], lambda: derive_k(0, 2), lambda: derive_k(0, 3)])

    for g in range(n_groups):
        t0 = g * GT
        xq = "sp" if g == 0 else "act"
        for c in range(DC):
            dma(xq, X.t[:, c, :], xT[c * 128:(c + 1) * 128, t0:t0 + GT], (), [X.buf(c)],
                f"xin{c}")
        if n_layers > N_A:
            dma("sp", ROPE.t[:, :, :], rope_d.rearrange("a p t -> p a t")[:, :, t0:t0 + GT], (),
                [ROPE.buf()], "rope")
        for l in range(n_layers):
            if g == 0 and l + 1 < n_layers:
                nl = l + 1
                parts = ada_layer_parts(nl)
                tail = [lambda nl=nl: derive_layer(nl)]
                if nl == N_A:
                    parts += kv_parts()
                    tail.append(derive_kv)
                micro.extend(schedule_parts(parts, tail))
            if l < N_A:
                conv_mixer(l, g)
            else:
                rs = None
                if l == N_A:
                    pe_fill(NFILL)
                    for c in range(DC):
                        sq_accum(X.t[:, c, :], [X.buf(c)], c, DC)
                    rs = rs_finish(D * EPS)
                    shared_kv(g, rs)
                attention(l, g, rs)
            mlp(l, g)
            pump(10 ** 6)
        for c in range(DC):
            dma("sp", outT[c * 128:(c + 1) * 128, t0:t0 + GT], X.t[:, c, :], [X.buf(c)], [],
                f"xout{c}")
    fins = []
    for c in range(DC):
        fb = Buf(f"fin{c}")
        fb.w = ("d", f"xout{c}", P.dma_val[f"xout{c}"])
        fins.append(fb)
    P.emit("sp", None, fins, [])
    with nc.allow_low_precision("bf16 matmul operands, fp32 accumulation"):
        P.lower(nc)
    return nc


def prep_inputs(inp):
    f = lambda a: np.ascontiguousarray(np.asarray(a, np.float32))
    x = f(inp["x"])
    B = x.shape[0]
    ci = f(inp["conv_in_w"])
    bg, cg, xh = ci[:, :, 0:D], ci[:, :, D:2 * D], ci[:, :, 2 * D:3 * D]
    parts = []
    for c in range(DC):
        sl = slice(c * 128, (c + 1) * 128)
        parts += [cg[:, :, sl], xh[:, :, sl], bg[:, :, sl]]
    conv_in_r = np.ascontiguousarray(np.concatenate(parts, axis=2))
    kva = f(inp["kv_a_w"])
    pe = kva[:, KV_LORA:KV_LORA + 64]
    sw = np.concatenate([pe[:, 32:64], pe[:, 0:32]], axis=1)
    kv_a_ext = np.ascontiguousarray(np.concatenate([kva[:, :KV_LORA], pe, pe, sw, sw], axis=1))
    qb = f(inp["q_b_w"])
    qparts = []
    for hp in range(4):
        h0, h1 = 2 * hp, 2 * hp + 1
        n0 = qb[:, :, h0 * 192:h0 * 192 + 128]
        n1 = qb[:, :, h1 * 192:h1 * 192 + 128]
        p0 = qb[:, :, h0 * 192 + 128:h0 * 192 + 192]
        p1 = qb[:, :, h1 * 192 + 128:h1 * 192 + 192]
        s0 = np.concatenate([p0[:, :, 32:64], p0[:, :, 0:32]], axis=2)
        s1 = np.concatenate([p1[:, :, 32:64], p1[:, :, 0:32]], axis=2)
        qparts += [n0, n1, p0, p1, s0, s1]
    q_b_ext = np.ascontiguousarray(np.concatenate(qparts, axis=2))
    rope = _rope_tables()
    shared = dict(
        rope=rope, ident=np.eye(128, dtype=np.float32), ada_w=f(inp["ada_w"]), kv_ada_w=f(inp["kv_ada_w"]), conv_in_r=conv_in_r,
        conv_out_w=f(inp["conv_out_w"]), kv_a_ext=kv_a_ext, kv_b_w=f(inp["kv_b_w"]),
        q_a_w=f(inp["q_a_w"]), q_b_ext=q_b_ext, attn_o_w=f(inp["attn_o_w"]),
        mlp_up_w=f(inp["mlp_up_w"]), mlp_down_w=f(inp["mlp_down_w"]))
    sm = np.zeros((128, SP_N), np.float32)

    def put(name, v):
        a = _fm(v)
        sm[:, SP_COLS[name]:SP_COLS[name] + a.shape[1]] = a

    ng = f(inp["norm_g"])
    adab = f(inp["ada_b"])
    for l in range(DEPTH):
        for i in range(4):
            put(f"ng{l}_{i}", ng[l, i])
        put(f"adab{l}", adab[l])
    cw = f(inp["conv_w"])
    cbv = f(inp["conv_b"])
    for l in range(N_A):
        for t in range(3):
            put(f"cw{l}_{t}", cw[l, t])
        put(f"cb{l}", cbv[l])
    put("kvadab", f(inp["kv_ada_b"]))
    put("kvng", f(inp["kv_norm_g"]))
    put("kvang", f(inp["kv_a_norm_g"]))
    qg = f(inp["q_a_norm_g"])
    for j in range(DEPTH - N_A):
        put(f"qang{j}", qg[j])
    c = f(inp["c"])
    in_maps = []
    for b in range(B):
        smb = sm.copy()
        smb[:, SP_COLS["c"]:SP_COLS["c"] + 8] = _fm(c[b])
        m = dict(shared)
        m["xT"] = np.ascontiguousarray(x[b].T)
        m["smallp"] = smb
        in_maps.append(m)
    return in_maps


_NC_CACHE = {}


def kernel(**inputs):
    in_maps = prep_inputs(inputs)
    if "nc" not in _NC_CACHE:
        _NC_CACHE["nc"] = build()
    nc = _NC_CACHE["nc"]
    res = run_bass_kernel_spmd(nc, in_maps, core_ids=list(range(len(in_maps))))
    out = np.stack([np.ascontiguousarray(r["outT"].T) for r in res.results], axis=0)
    return out.astype(np.float32)
```
